# Optimizing a Trainium2 kernel written in Bass

```python
import math
import jax, jax.numpy as jnp
from jax import lax
import numpy as np

D_MODEL = 1024
BATCH = 8
SEQ = 2048
DEPTH = 1
DEC_BATCH = 2
DEC_SEQ = 16384
PAST_LEN = 128

CONV_WIDTH = D_MODEL // 2
CONV_GROUPS = 8
CONV_TAPS = 3
N_HEADS = 8
QK_NOPE_DIM = 64
QK_ROPE_DIM = 32
V_HEAD_DIM = 64
ATTN_WIDTH = N_HEADS * V_HEAD_DIM
Q_LORA_RANK = 384
KV_LORA_RANK = 256
MIX_WIDTH = CONV_WIDTH + ATTN_WIDTH
ROPE_THETA = 10000.0
Q_BLOCK = 128
NORM_EPS = 1e-6
SPLIT_SIZES = (CONV_WIDTH, CONV_WIDTH, CONV_WIDTH, CONV_WIDTH,
               Q_LORA_RANK, KV_LORA_RANK, QK_ROPE_DIM, ATTN_WIDTH)
IN_WIDTH = 4 * CONV_WIDTH + Q_LORA_RANK + KV_LORA_RANK + QK_ROPE_DIM + ATTN_WIDTH

kernel_name = "hymba_shortconv_mla_sandwich_encoder"


def _rmsnorm(x, g):
    xf = x.astype(jnp.float32)
    y = xf * lax.rsqrt(jnp.mean(xf * xf, axis=-1, keepdims=True) + NORM_EPS)
    return (y * g.astype(jnp.float32)).astype(x.dtype)


def _split_points():
    pts, acc = [], 0
    for s in SPLIT_SIZES[:-1]:
        acc += s
        pts.append(acc)
    return pts


def _rope_tables(seq_len, dtype):
    freqs = 1.0 / (ROPE_THETA ** (jnp.arange(0, QK_ROPE_DIM, 2, dtype=jnp.float32) / QK_ROPE_DIM))
    ang = jnp.arange(seq_len, dtype=jnp.float32)[:, None] * freqs[None, :]
    return jnp.cos(ang).astype(dtype), jnp.sin(ang).astype(dtype)


def _apply_rope(x, cos, sin):
    x1, x2 = jnp.split(x, 2, axis=-1)
    return jnp.concatenate([x1 * cos - x2 * sin, x2 * cos + x1 * sin], axis=-1)


def _short_conv(u, w):
    up = jnp.pad(u, ((0, 0), (1, 1), (0, 0)))
    return up[:, :-2] * w[0] + up[:, 1:-1] * w[1] + up[:, 2:] * w[2]


def _attention(q, k, v):
    b, s, h, dqk = q.shape
    dv = v.shape[-1]
    nblk = s // Q_BLOCK
    scale = 1.0 / math.sqrt(dqk)
    qb = q.reshape(b, nblk, Q_BLOCK, h, dqk).transpose(1, 0, 2, 3, 4)

    def one_block(qblk):
        sc = jnp.einsum('bqhd,bkhd->bhqk', qblk, k, preferred_element_type=jnp.float32) * scale
        p = jax.nn.softmax(sc, axis=-1)
        return jnp.einsum('bhqk,bkhd->bqhd', p.astype(v.dtype), v)

    o = lax.map(one_block, qb)
    return o.transpose(1, 0, 2, 3, 4).reshape(b, s, h * dv)


def _layer(x, norm_pre, w_in, conv_w, q_norm, w_uq, kv_norm, w_ukv, w_out, norm_post):
    b, s, _ = x.shape
    hdn = _rmsnorm(x, norm_pre)
    proj = jnp.einsum('bsd,de->bse', hdn, w_in)
    u, b_gate, c_gate, z_conv, q_lat, kv_lat, k_pe, z_attn = jnp.split(proj, _split_points(), axis=-1)

    conv_out = b_gate * _short_conv(c_gate * u, conv_w) * jax.nn.silu(z_conv)

    cos, sin = _rope_tables(s, x.dtype)
    q = jnp.einsum('bsr,re->bse', _rmsnorm(q_lat, q_norm), w_uq).reshape(
        b, s, N_HEADS, QK_NOPE_DIM + QK_ROPE_DIM)
    q_nope, q_pe = jnp.split(q, [QK_NOPE_DIM], axis=-1)
    q_pe = _apply_rope(q_pe, cos[:, None, :], sin[:, None, :])
    kv = jnp.einsum('bsr,re->bse', _rmsnorm(kv_lat, kv_norm), w_ukv).reshape(
        b, s, N_HEADS, QK_NOPE_DIM + V_HEAD_DIM)
    k_nope, v = jnp.split(kv, [QK_NOPE_DIM], axis=-1)
    k_pe = _apply_rope(k_pe, cos, sin)
    k_pe = jnp.broadcast_to(k_pe[:, :, None, :], (b, s, N_HEADS, QK_ROPE_DIM))
    q_full = jnp.concatenate([q_nope, q_pe], axis=-1)
    k_full = jnp.concatenate([k_nope, k_pe], axis=-1)
    attn_out = _attention(q_full, k_full, v) * jax.nn.silu(z_attn)

    mix = jnp.concatenate([conv_out, attn_out], axis=-1)
    out = jnp.einsum('bse,ed->bsd', mix, w_out)
    return x + _rmsnorm(out, norm_post)


def setup_inputs(seed: int = 0) -> dict:
    key = jax.random.key(seed)
    ks = jax.random.split(key, 12)
    f32 = jnp.float32
    nrm = lambda k, shp, scale: (jax.random.normal(k, shp, f32) * scale)
    return {
        "x_prompt": nrm(ks[0], (BATCH, SEQ, D_MODEL), 1.0),
        "x_sample": nrm(ks[1], (DEC_BATCH, DEC_SEQ, D_MODEL), 1.0),
        "norm_pre": 1.0 + nrm(ks[2], (DEPTH, D_MODEL), 0.02),
        "w_in": nrm(ks[3], (DEPTH, D_MODEL, IN_WIDTH), D_MODEL ** -0.5),
        "conv_w": nrm(ks[4], (DEPTH, CONV_TAPS, CONV_WIDTH), CONV_TAPS ** -0.5),
        "q_norm": 1.0 + nrm(ks[5], (DEPTH, Q_LORA_RANK), 0.02),
        "w_uq": nrm(ks[6], (DEPTH, Q_LORA_RANK, N_HEADS * (QK_NOPE_DIM + QK_ROPE_DIM)), Q_LORA_RANK ** -0.5),
        "kv_norm": 1.0 + nrm(ks[7], (DEPTH, KV_LORA_RANK), 0.02),
        "w_ukv": nrm(ks[8], (DEPTH, KV_LORA_RANK, N_HEADS * (QK_NOPE_DIM + V_HEAD_DIM)), KV_LORA_RANK ** -0.5),
        "w_out": nrm(ks[9], (DEPTH, MIX_WIDTH, D_MODEL), MIX_WIDTH ** -0.5),
        "norm_post": 1.0 + nrm(ks[10], (DEPTH, D_MODEL), 0.02),
    }


def reference(x_prompt, x_sample, norm_pre, w_in, conv_w, q_norm, w_uq, kv_norm, w_ukv, w_out, norm_post):
    y_prompt = x_prompt
    y_sample = x_sample
    for l in range(DEPTH):
        y_prompt = _layer(y_prompt, norm_pre[l], w_in[l], conv_w[l], q_norm[l], w_uq[l],
                          kv_norm[l], w_ukv[l], w_out[l], norm_post[l])
        y_sample = _layer(y_sample, norm_pre[l], w_in[l], conv_w[l], q_norm[l], w_uq[l],
                          kv_norm[l], w_ukv[l], w_out[l], norm_post[l])
    return (y_prompt, y_sample)
```

```python
import contextlib
import math
import numpy as np
import concourse.bass as bass
import concourse.mybir as mybir
from concourse.bass_utils import run_bass_kernel_spmd

F32 = mybir.dt.float32
BF16 = mybir.dt.bfloat16
ALU = mybir.AluOpType
AF = mybir.ActivationFunctionType

PE, ACT, DVE, POOL, SP = "pe", "act", "dve", "pool", "sp"
COMPUTE = (PE, ACT, DVE, POOL)
EPS = 1e-6
D = 1024
NCORES = 8
S_P, S_S, NQ_S = 2048, 16384, 4096
QB = 2048


class Buf:
    __slots__ = ("name", "excl", "lw", "rd")

    def __init__(self, name, excl=False):
        self.name = name
        self.excl = excl
        self.lw = None
        self.rd = {}


class Tracker:
    DMA_RING = 8

    def __init__(self, nc):
        self.nc = nc
        self.ops = []
        self.sems = {e: nc.alloc_semaphore(name=f"s_{e}") for e in COMPUTE}
        self.dma_sems = {q: [nc.alloc_semaphore(name=f"d_{q}_{i}") for i in range(self.DMA_RING)]
                         for q in (SP, POOL)}
        self.sig_cnt = {e: 0 for e in COMPUTE}
        self.dma_cnt = {q: 0 for q in (SP, POOL)}
        self.waited = {e: {p: 0 for p in COMPUTE} for e in (PE, ACT, DVE, POOL, SP)}
        self.dma_waited = {e: set() for e in (PE, ACT, DVE, POOL, SP)}
        self.flushed = 0
        self.last_dma = {}
        self.bar_cnt = None
        self.bar_dma = None

    def _rec(self, eng, fn, reads, writes, dma=False):
        i = len(self.ops)
        deps = set()
        for b in reads:
            if b.lw is not None:
                deps.add(b.lw)
            if b.excl:
                deps.update(b.rd.values())
        for b in writes:
            if b.lw is not None:
                deps.add(b.lw)
            deps.update(b.rd.values())
        key = eng if not dma else ("dma", i)
        for b in reads:
            if b.excl:
                b.lw = i
                b.rd = {}
            else:
                b.rd[key] = i
        for b in writes:
            b.lw = i
            b.rd = {}
        deps.discard(i)
        self.ops.append(dict(eng=eng, fn=fn, deps=deps, dma=dma, need=False))
        return i

    def op(self, eng, fn, reads=(), writes=()):
        return self._rec(eng, fn, reads, writes)

    def dma(self, q, fn, reads=(), writes=()):
        return self._rec(q, fn, reads, writes, dma=True)

    def flush(self, final=False):
        nc = self.nc
        ops = self.ops
        lo, hi = self.flushed, len(ops)
        for i in range(lo, hi):
            o = ops[i]
            for d in o["deps"]:
                p = ops[d]
                if p["dma"]:
                    continue
                if p["eng"] != o["eng"] or o["dma"] or p["eng"] != PE:
                    p["need"] = True
        _per = {e: [] for e in COMPUTE}
        for i in range(lo, hi):
            if not ops[i]["dma"]:
                _per[ops[i]["eng"]].append(i)
        for e in COMPUTE:
            if _per[e]:
                ops[_per[e][-1]]["need"] = True
        for i in range(lo, hi):
            o = ops[i]
            if o["dma"]:
                q = o["eng"]
                n = self.dma_cnt[q]
                self.dma_cnt[q] += 1
                o["ring"] = n % self.DMA_RING
                o["val"] = 16 * (n // self.DMA_RING + 1)
            elif o["need"]:
                self.sig_cnt[o["eng"]] += 1
                o["cnt"] = self.sig_cnt[o["eng"]]
        per = {e: [] for e in (PE, ACT, DVE, POOL, SP)}
        for i in range(lo, hi):
            per[ops[i]["eng"]].append(i)
        bar_cnt, bar_dma = self.bar_cnt, self.bar_dma

        def run(ename, eobj):
            waited = self.waited[ename]
            dwaited = self.dma_waited[ename]
            if bar_cnt is not None and per[ename]:
                for p_, c_ in bar_cnt.items():
                    if c_ > 0 and waited[p_] < c_:
                        eobj.wait_ge(self.sems[p_], c_)
                        waited[p_] = c_
                for (q_, r_), v_ in sorted(bar_dma.items()):
                    eobj.wait_ge(self.dma_sems[q_][r_], v_)
            for i in per[ename]:
                o = ops[i]
                for d in sorted(o["deps"]):
                    p = ops[d]
                    if p["dma"]:
                        if d not in dwaited:
                            eobj.wait_ge(self.dma_sems[p["eng"]][p["ring"]], p["val"])
                            dwaited.add(d)
                    else:
                        if p["eng"] == ename and ename == PE and not o["dma"]:
                            continue
                        c = p.get("cnt")
                        if c is None:
                            continue
                        if waited[p["eng"]] < c:
                            eobj.wait_ge(self.sems[p["eng"]], c)
                            waited[p["eng"]] = c
                if o["dma"]:
                    sem = self.dma_sems[ename][o["ring"]]
                    if o["val"] > 16:
                        eobj.wait_ge(sem, o["val"] - 16)
                    ins = o["fn"](eobj)
                    ins.then_inc(sem, 16)
                    self.last_dma[(ename, o["ring"])] = o["val"]
                else:
                    ins = o["fn"](eobj)
                    if o["need"]:
                        ins.then_inc(self.sems[ename], 1)
                o["fn"] = None
            if final and ename == SP:
                for (q, r), v in sorted(self.last_dma.items()):
                    eobj.wait_ge(self.dma_sems[q][r], v)

        with nc.Block() as block:
            block.tensor(lambda e: run(PE, e))
            block.scalar(lambda e: run(ACT, e))
            block.vector(lambda e: run(DVE, e))
            block.gpsimd(lambda e: run(POOL, e))
            block.sync(lambda e: run(SP, e))
        self.flushed = hi
        self.bar_cnt = dict(self.sig_cnt)
        self.bar_dma = dict(self.last_dma)


class KB:
    def __init__(self, nc):
        self.nc = nc
        self.T = Tracker(nc)

    def mm(self, out, lhsT, rhs, start, stop, reads, writes, tile_position=None):
        if tile_position is None:
            self.T.op(PE, lambda e: e.matmul(out, lhsT=lhsT, rhs=rhs, start=start, stop=stop), reads, writes)
        else:
            self.T.op(PE, lambda e: e.matmul(out, lhsT=lhsT, rhs=rhs, start=start, stop=stop,
                                             tile_position=tile_position), reads, writes)

    def tr(self, out, in_, ident, reads, writes):
        self.T.op(PE, lambda e: e.transpose(out, in_, ident), reads, writes)

    def act(self, out, in_, func, reads, writes, scale=1.0, bias=None, accum=None):
        def f(e):
            kw = dict(out=out, in_=in_, func=func, scale=scale)
            if bias is not None:
                kw["bias"] = bias
            if accum is not None:
                kw["accum_out"] = accum
            return e.activation(**kw)
        self.T.op(ACT, f, reads, writes)

    def tt(self, eng, out, in0, in1, op, reads, writes):
        self.T.op(eng, lambda e: e.tensor_tensor(out=out, in0=in0, in1=in1, op=op), reads, writes)

    def ts(self, eng, out, in0, s1, op0, reads, writes):
        self.T.op(eng, lambda e: e.tensor_scalar(out=out, in0=in0, scalar1=s1, scalar2=None, op0=op0), reads, writes)

    def stt(self, out, in0, scalar, in1, op0, op1, reads, writes):
        self.T.op(DVE, lambda e: e.scalar_tensor_tensor(out=out, in0=in0, scalar=scalar, in1=in1, op0=op0, op1=op1),
                  reads, writes)

    def cp(self, eng, out, in_, reads, writes):
        if eng == ACT:
            self.act(out, in_, AF.Copy, reads, writes)
        else:
            self.T.op(eng, lambda e: e.tensor_copy(out=out, in_=in_), reads, writes)

    def recip(self, out, in_, reads, writes):
        self.T.op(DVE, lambda e: e.reciprocal(out=out, in_=in_), reads, writes)

    def memset(self, eng, ap, val, writes):
        self.T.op(eng, lambda e: e.memset(ap, val), (), writes)

    def dma(self, q, out, in_, reads, writes):
        self.T.dma(q, lambda e: e.dma_start(out=out, in_=in_), reads, writes)

    def load_fold(self, dst, Bdst, src, nk, ncols, scale_ap, stg, Bstg, col0=0, ctr=None):
        ctr = ctr if ctr is not None else [0]
        for k in range(nk):
            i = ctr[0]
            ctr[0] += 1
            s = i % len(stg)
            st_ap = stg[s][:, 0:ncols]
            self.dma(SP, st_ap, src[k * 128:(k + 1) * 128, col0:col0 + ncols], (), [Bstg[s]])
            d_ap = dst[:, k, col0:col0 + ncols]
            if i % 2 == 0:
                if scale_ap is None:
                    self.cp(DVE, d_ap, st_ap, [Bstg[s]], [Bdst])
                else:
                    self.ts(DVE, d_ap, st_ap, scale_ap[:, k:k + 1], ALU.mult, [Bstg[s]], [Bdst])
            else:
                if scale_ap is None:
                    self.act(d_ap, st_ap, AF.Copy, [Bstg[s]], [Bdst])
                else:
                    self.act(d_ap, st_ap, AF.Copy, [Bstg[s]], [Bdst], scale=scale_ap[:, k:k + 1])

    def lnt_tile(self, C, x_rows, xt, Bxt, xn, Bxn, xnT, BxnT, tcol, trb, Btr, evac, i, nrows=128):
        ssc = C.ss[:, (i % 8):(i % 8) + 1]
        rsc = C.rs[:, (i % 8):(i % 8) + 1]
        Bss, Brs = C.Bss[i % 8], C.Brs[i % 8]
        self.dma(SP, xt[0:nrows, :], x_rows, (), [Bxt])
        self.act(xn, xt, AF.Square, [Bxt], [Bxn, Bss], accum=ssc)
        self.act(rsc, ssc, AF.Ln, [Bss], [Brs], scale=1.0 / D, bias=C.epsb[:, 0:1])
        self.act(rsc, rsc, AF.Exp, [Brs], [Brs], scale=-0.5)
        self.ts(DVE, xn, xt, rsc, ALU.mult, [Bxt, Brs], [Bxn])
        trv = trb.bitcast(BF16)
        for k in range(8):
            self.tr(trv[:, k * 128:(k + 1) * 128], xn[:, k * 128:(k + 1) * 128], C.ident[:], [Bxn, C.Bconst], [Btr])
        self.cp(evac, xnT[:, :, tcol * 128:(tcol + 1) * 128], trv.rearrange("p (k t) -> p k t", k=8), [Btr], [BxnT])

    def rstd_bcast(self, C, banks, Bbanks, nm, sq, Bsq, ssb, Bssb, R, BR, dim):
        for m in range(nm):
            self.act(sq[:, m, :], banks[m], AF.Square, [Bbanks[m]], [Bsq])
        for m in range(nm):
            self.mm(ssb, C.ones[:], sq[:, m, :], m == 0, m == nm - 1, [Bsq, C.Bconst], [Bssb])
        self.act(R[:], ssb, AF.Ln, [Bssb], [BR], scale=1.0 / dim, bias=C.epsb[:, 0:1])
        self.act(R[:], R[:], AF.Exp, [BR], [BR], scale=-0.5)


def pipeline(n, stages):
    ns = len(stages)
    for it in range(n + ns - 1):
        for si in range(ns - 1, -1, -1):
            i = it - si
            if 0 <= i < n:
                stages[si](i)


def build_program():
    nc = bass.Bass("TRN2", target_bir_lowering=False)
    K = KB(nc)
    T = K.T

    def din(name, shape):
        return nc.dram_tensor(name, list(shape), F32, kind="ExternalInput").ap()

    xp = din("xp", [S_P, D])
    xs = din("xs", [S_S, D])
    xq = din("xq", [NQ_S, D])
    xh = din("xh", [24, D])
    rope = din("rope", [2, 32, S_S])
    ropeq = din("ropeq", [2, 128, NQ_S])
    ropep4 = din("ropep4", [2, 128, S_P])
    w_kvp_d = din("w_kvp", [D, 320])
    w_ql_d = din("w_ql", [D, 384])
    w_uqp_d = din("w_uqp", [384, 1024])
    w_ukvp_d = din("w_ukvp", [256, 1024])
    w3_d = din("w3", [D, 2560])
    w_out_d = din("w_out", [D, D])
    npre_d = din("npre", [128, 8])
    qnv_d = din("qnv", [128, 3])
    kvv_d = din("kvv", [128, 2])
    convw_d = din("convw", [128, 12])
    npost_d = din("npost", [1, D])
    ident_d = din("ident", [128, 128])
    yp = nc.dram_tensor("yp", [S_P, D], F32, kind="ExternalOutput").ap()
    ys = nc.dram_tensor("ys", [NQ_S, D], F32, kind="ExternalOutput").ap()

    es = contextlib.ExitStack()
    uid = [0]

    def sb(stack, name, shape, dt):
        uid[0] += 1
        return stack.enter_context(nc.sbuf_tensor(f"{name}_u{uid[0]}", list(shape), dt))

    class C:
        pass

    class LNT:
        def __init__(self, ph, nslots):
            self.n = nslots
            self.xt = [sb(ph, "xt", [128, D], F32) for _ in range(nslots)]
            self.xn = [sb(ph, "xn", [128, D], BF16) for _ in range(nslots)]
            self.Bxt = [Buf("xt") for _ in range(nslots)]
            self.Bxn = [Buf("xn") for _ in range(nslots)]
            self.ctr = 0

        def front(self, rows, nrows=128, pre_zero=False):
            i = self.ctr
            self.ctr += 1
            s = i % self.n
            xt, xn, Bxt, Bxn = self.xt[s], self.xn[s], self.Bxt[s], self.Bxn[s]
            c8 = i % 8
            ssc, rsc = C.ss[:, c8:c8 + 1], C.rs[:, c8:c8 + 1]
            if pre_zero:
                K.memset(DVE, xt[:], 0.0, [Bxt])
            K.dma(SP, xt[0:nrows, :], rows, (), [Bxt])
            K.act(xn[:], xt[:], AF.Square, [Bxt], [Bxn, C.Bss[c8]], accum=ssc)
            K.act(rsc, ssc, AF.Ln, [C.Bss[c8]], [C.Brs[c8]], scale=1.0 / D, bias=C.epsb[:, 0:1])
            K.act(rsc, rsc, AF.Exp, [C.Brs[c8]], [C.Brs[c8]], scale=-0.5)
            K.ts(DVE, xn[:], xt[:], rsc, ALU.mult, [Bxt, C.Brs[c8]], [Bxn])
            return s

        def back(self, s, xnT, BxnT, tcol, trb, Btr):
            xn, Bxn = self.xn[s], self.Bxn[s]
            trv = trb.bitcast(BF16)
            for k in range(8):
                K.tr(trv[:, k * 128:(k + 1) * 128], xn[:, k * 128:(k + 1) * 128], C.ident[:], [Bxn, C.Bconst], [Btr])
            K.cp(DVE, xnT[:, :, tcol * 128:(tcol + 1) * 128], trv.rearrange("p (k t) -> p k t", k=8), [Btr], [BxnT])

        def chunk_front(self, row_fn):
            return [self.front(row_fn(t)) for t in range(4)]

        def chunk_back(self, slots, xnT, BxnT, trbs, Btrs):
            for t in range(4):
                self.back(slots[t], xnT, BxnT, t, trbs[t % len(trbs)], Btrs[t % len(trbs)])

        def chunk(self, row_fn, xnT, BxnT, trbs, Btrs):
            prev = None
            for t in range(4):
                s = self.front(row_fn(t))
                if prev is not None:
                    self.back(prev, xnT, BxnT, t - 1, trbs[(t - 1) % len(trbs)], Btrs[(t - 1) % len(trbs)])
                prev = s
            self.back(prev, xnT, BxnT, 3, trbs[3 % len(trbs)], Btrs[3 % len(trbs)])

    def rstd_tail(ssb, Bssb, R, BR, dim):
        K.act(R[:], ssb, AF.Ln, [Bssb], [BR], scale=1.0 / dim, bias=C.epsb[:, 0:1])
        K.act(R[:], R[:], AF.Exp, [BR], [BR], scale=-0.5)

    with es:
        pp = [es.enter_context(nc.psum_tensor(f"pp{i}", [128, 1024], F32)) for i in range(4)]
        bank = [pp[i // 2][:, (i % 2) * 512:(i % 2) * 512 + 512] for i in range(8)]
        Bbank = [Buf(f"bank{i}", excl=True) for i in range(8)]

        C.ident = sb(es, "ident", [128, 128], BF16)
        C.ones = sb(es, "ones", [128, 128], BF16)
        C.onesf = sb(es, "onesf", [128, 128], F32)
        C.epsb = sb(es, "epsb", [128, 1], F32)
        C.npre = sb(es, "npre", [128, 8], F32)
        C.qnv = sb(es, "qnv", [128, 3], F32)
        C.kvv = sb(es, "kvv", [128, 2], F32)
        C.convw = sb(es, "convw", [128, 12], F32)
        C.ss = sb(es, "ss", [128, 8], F32)
        C.rs = sb(es, "rs", [128, 8], F32)
        C.Bconst = Buf("const")
        C.Bss = [Buf(f"ss{i}") for i in range(8)]
        C.Brs = [Buf(f"rs{i}") for i in range(8)]

        with contextlib.ExitStack() as ph:
            idf = sb(ph, "idf", [128, 128], F32)
            Bidf = Buf("idf")
            K.dma(SP, idf[:], ident_d[:, :], (), [Bidf])
            K.dma(SP, C.npre[:], npre_d[:, :], (), [C.Bconst])
            K.dma(SP, C.qnv[:], qnv_d[:, :], (), [C.Bconst])
            K.dma(SP, C.kvv[:], kvv_d[:, :], (), [C.Bconst])
            K.dma(SP, C.convw[:], convw_d[:, :], (), [C.Bconst])
            K.cp(DVE, C.ident[:], idf[:], [Bidf], [C.Bconst])
            K.memset(DVE, C.ones[:], 1.0, [C.Bconst])
            K.memset(DVE, C.onesf[:], 1.0, [C.Bconst])
            K.memset(DVE, C.epsb[:], EPS, [C.Bconst])
            K.ts(DVE, C.qnv[:], C.qnv[:], 1.0 / math.sqrt(96.0), ALU.mult, [C.Bconst], [C.Bconst])
            T.flush()

        jobs = [
            dict(name="p", x=xp, xq=xp, S=S_P, NQ=S_P, rope=rope, ropeq=ropep4, halo=0, y=yp),
            dict(name="s", x=xs, xq=xq, S=S_S, NQ=NQ_S, rope=rope, ropeq=ropeq, halo=8, y=ys),
        ]
        for job in jobs:
            S, NQ = job["S"], job["NQ"]
            jn = job["name"]
            NCK = S // 512
            NT = S // 128
            with contextlib.ExitStack() as js:
                oT = sb(js, f"oT_{jn}", [128, 4, NQ], BF16)
                BoT = [Buf(f"oT{c}") for c in range(NQ // 512)]
                with contextlib.ExitStack() as ks:
                    kvnT = sb(ks, f"kvnT_{jn}", [128, 2, S], BF16)
                    KT = sb(ks, f"KT_{jn}", [128, S], BF16)
                    Bkvn = [Buf(f"kvn{c}") for c in range(NCK)]
                    BKTr = [Buf(f"ktr{c}") for c in range(NCK)]
                    BKTn = [Buf(f"ktn{c}") for c in range(NCK)]
                    Bwukv = Buf("wukv")

                    with contextlib.ExitStack() as ph:
                        w_kvp = sb(ph, "w_kvp", [128, 8, 320], BF16)
                        Bwkvp = Buf("wkvp")
                        L = LNT(ph, 3)
                        xnT = [sb(ph, f"xnT{i}", [128, 8, 512], BF16) for i in range(2)]
                        BxnT = [Buf(f"xnT{i}") for i in range(2)]
                        rc = [sb(ph, f"rc{i}", [32, 2, 512], F32) for i in range(2)]
                        Brc = [Buf(f"rc{i}") for i in range(2)]
                        sq = sb(ph, "sq", [128, 2, 512], BF16)
                        Bsq = Buf("sq")
                        R = sb(ph, "R", [128, 512], F32)
                        BR = Buf("R")
                        t1 = sb(ph, "t1", [32, 512], F32)
                        t2 = sb(ph, "t2", [32, 512], F32)
                        Bt1, Bt2 = Buf("t1"), Buf("t2")
                        K.load_fold(w_kvp, Bwkvp, w_kvp_d, 8, 320, C.npre, [t_[:] for t_ in L.xt] + [R[:]], L.Bxt + [BR])

                        def kv_lnt(c):
                            L.chunk(lambda t: job["x"][c * 512 + t * 128:c * 512 + (t + 1) * 128, :], xnT[c % 2],
                                    BxnT[c % 2], bank[0:2], Bbank[0:2])

                        def kv_s3(c):
                            s = c % 2
                            K.dma(SP, rc[s][:, 0, :], job["rope"][0, :, c * 512:(c + 1) * 512], (), [Brc[s]])
                            K.dma(SP, rc[s][:, 1, :], job["rope"][1, :, c * 512:(c + 1) * 512], (), [Brc[s]])
                            for m in range(2):
                                for k in range(8):
                                    K.mm(bank[2 + 2 * s + m], w_kvp[:, k, m * 128:(m + 1) * 128], xnT[s][:, k, :], k == 0,
                                         k == 7, [Bwkvp, BxnT[s]], [Bbank[2 + 2 * s + m]])
                            for k in range(8):
                                K.mm(bank[6][0:64, :], w_kvp[:, k, 256:320], xnT[s][:, k, :], k == 0, k == 7,
                                     [Bwkvp, BxnT[s]], [Bbank[6]])

                        def kv_s4a(c):
                            s = c % 2
                            cs = slice(c * 512, (c + 1) * 512)
                            for m in range(2):
                                K.act(sq[:, m, :], bank[2 + 2 * s + m], AF.Square, [Bbank[2 + 2 * s + m]], [Bsq])
                            K.tt(DVE, t1[:], bank[6][0:32, :], rc[s][:, 0, :], ALU.mult, [Bbank[6], Brc[s]], [Bt1])
                            K.tt(DVE, t2[:], bank[6][32:64, :], rc[s][:, 1, :], ALU.mult, [Bbank[6], Brc[s]], [Bt2])
                            K.tt(DVE, KT[0:32, cs], t1[:], t2[:], ALU.add, [Bt1, Bt2], [BKTr[c]])
                            K.memset(POOL, KT[32:64, cs], 0.0, [BKTr[c]])

                        def kv_s4b(c):
                            s = c % 2
                            cs = slice(c * 512, (c + 1) * 512)
                            for m in range(2):
                                K.mm(bank[7], C.ones[:], sq[:, m, :], m == 0, m == 1, [Bsq, C.Bconst], [Bbank[7]])
                            rstd_tail(bank[7], Bbank[7], R, BR, 256.0)
                            for m in range(2):
                                K.tt(DVE, kvnT[:, m, cs], bank[2 + 2 * s + m], R[:], ALU.mult,
                                     [Bbank[2 + 2 * s + m], BR], [Bkvn[c]])

                        pipeline(NCK, [kv_lnt, kv_s3, kv_s4a, kv_s4b])
                        T.flush()

                    for blk in range(NQ // QB):
                        with contextlib.ExitStack() as bs:
                            QT = sb(bs, f"QT_{jn}{blk}", [128, 8, QB], BF16)
                            BQT = [[Buf(f"QT{c}_{h_}") for h_ in range(8)] for c in range(QB // 512)]
                            with contextlib.ExitStack() as ph:
                                w_ql = sb(ph, "w_ql", [128, 8, 384], BF16)
                                w_uqp = sb(ph, "w_uqp", [128, 3, 1024], BF16)
                                Bwql, Bwuqp = Buf("wql"), Buf("wuqp")
                                L = LNT(ph, 2)
                                xnT0 = sb(ph, "xnT0", [128, 8, 512], BF16)
                                BxnT0 = Buf("xnT0")
                                rc0 = sb(ph, "rc0", [128, 2, 512], F32)
                                Brc0 = Buf("rc0")
                                sq = sb(ph, "sq", [128, 3, 512], BF16)
                                Bsq = Buf("sq")
                                qnT, BqnT = sq, Bsq
                                R = sb(ph, "R", [128, 512], F32)
                                BR = Buf("R")
                                t1 = sb(ph, "t1", [128, 512], F32)
                                t2 = sb(ph, "t2", [128, 512], F32)
                                Bt1, Bt2 = Buf("t1"), Buf("t2")
                                K.load_fold(w_ql, Bwql, w_ql_d, 8, 384, C.npre, [t_[:] for t_ in L.xt] + [R[:]], L.Bxt + [BR])
                                K.load_fold(w_uqp, Bwuqp, w_uqp_d, 3, 1024, C.qnv, [t_[:] for t_ in L.xt], L.Bxt)

                                def q_lnt(c):
                                    q0 = blk * QB + c * 512
                                    L.chunk(lambda t: job["xq"][q0 + t * 128:q0 + (t + 1) * 128, :], xnT0, BxnT0,
                                            bank[0:2], Bbank[0:2])

                                def q_s3(c):
                                    for m in range(3):
                                        for k in range(8):
                                            K.mm(bank[2 + m], w_ql[:, k, m * 128:(m + 1) * 128], xnT0[:, k, :], k == 0,
                                                 k == 7, [Bwql, BxnT0], [Bbank[2 + m]])

                                def q_s4(c):
                                    for m in range(3):
                                        K.act(sq[:, m, :], bank[2 + m], AF.Square, [Bbank[2 + m]], [Bsq])
                                    for m in range(3):
                                        K.mm(bank[5], C.ones[:], sq[:, m, :], m == 0, m == 2, [Bsq, C.Bconst], [Bbank[5]])
                                    rstd_tail(bank[5], Bbank[5], R, BR, 384.0)
                                    for m in range(3):
                                        K.tt(DVE, qnT[:, m, :], bank[2 + m], R[:], ALU.mult, [Bbank[2 + m], BR], [BqnT])

                                def q_s5(c):
                                    q0 = blk * QB + c * 512
                                    cs = slice(c * 512, (c + 1) * 512)
                                    K.memset(DVE, QT[32:64, :, cs], 0.0, BQT[c])
                                    K.dma(SP, rc0[:, 0, :], job["ropeq"][0, :, q0:q0 + 512], (), [Brc0])
                                    K.dma(SP, rc0[:, 1, :], job["ropeq"][1, :, q0:q0 + 512], (), [Brc0])
                                    for g in range(2):
                                        for bi_, blk_ in ((6, g), (7, 2 + g)):
                                            for m in range(3):
                                                K.mm(bank[bi_], w_uqp[:, m, blk_ * 128:(blk_ + 1) * 128], qnT[:, m, :],
                                                     m == 0, m == 2, [Bwuqp, BqnT], [Bbank[bi_]])
                                        K.tt(DVE, t1[:], bank[6], rc0[:, 0, :], ALU.mult, [Bbank[6], Brc0], [Bt1])
                                        K.tt(DVE, t2[:], bank[7], rc0[:, 1, :], ALU.mult, [Bbank[7], Brc0], [Bt2])
                                        for i in range(4):
                                            h = 4 * g + i
                                            K.tt(DVE, QT[0:32, h, cs], t1[32 * i:32 * i + 32, :], t2[32 * i:32 * i + 32, :],
                                                 ALU.add, [Bt1, Bt2], [BQT[c][h]])
                                    for p in range(4):
                                        bi_ = 6 + p % 2
                                        for m in range(3):
                                            K.mm(bank[bi_], w_uqp[:, m, (4 + p) * 128:(5 + p) * 128], qnT[:, m, :], m == 0, m == 2,
                                                 [Bwuqp, BqnT], [Bbank[bi_]])
                                        K.cp(ACT, QT[64:128, 2 * p, cs], bank[bi_][0:64, :], [Bbank[bi_]], [BQT[c][2 * p]])
                                        K.cp(ACT, QT[64:128, 2 * p + 1, cs], bank[bi_][64:128, :], [Bbank[bi_]],
                                             [BQT[c][2 * p + 1]])

                                pipeline(QB // 512, [q_lnt, q_s3, q_s4, q_s5])
                                T.flush()

                            with contextlib.ExitStack() as ph:
                                V = sb(ph, "V", [128, NT, 65], BF16)
                                BV = [Buf(f"V{g}") for g in range(NT // 8)]
                                Pb = [sb(ph, f"P{i}", [128, 1024], BF16) for i in range(4)]
                                BP = [Buf(f"P{i}") for i in range(4)]
                                Rhl = sb(ph, "Rhl", [128, 1024], BF16)
                                osb = sb(ph, "osb", [128, 512], F32)
                                Rs = sb(ph, "Rs", [128, 512], F32)
                                BRr, Bosb, BRs = Buf("Rr"), Buf("osb"), Buf("Rs")
                                K.memset(POOL, V[:, :, 64:65], 1.0, BV)
                                w_ukv = sb(ph, "w_ukv", [128, 2, 1024], BF16)
                                wi_ = 0
                                for m in range(2):
                                    for hf in range(2):
                                        stg_ = Pb[wi_][:].bitcast(F32)
                                        K.dma(SP, stg_, w_ukvp_d[m * 128:(m + 1) * 128, hf * 512:(hf + 1) * 512], (), [BP[wi_]])
                                        if wi_ % 2 == 0:
                                            K.ts(DVE, w_ukv[:, m, hf * 512:(hf + 1) * 512], stg_, C.kvv[:, m:m + 1], ALU.mult,
                                                 [BP[wi_]], [Bwukv])
                                        else:
                                            K.act(w_ukv[:, m, hf * 512:(hf + 1) * 512], stg_, AF.Copy, [BP[wi_]], [Bwukv],
                                                  scale=C.kvv[:, m:m + 1])
                                        wi_ += 1
                                NSC = 3
                                sc = [pp[0], pp[1], pp[2]]
                                Bsc = [[Bbank[0], Bbank[1]], [Bbank[2], Bbank[3]], [Bbank[4], Bbank[5]]]
                                gen, Bgen = bank[7], Bbank[7]
                                accb_ = [6, 7, 6, 6]
                                fbb_ = [6, 7, 7, 7]
                                pi = 0
                                pending = []
                                lgb_ = [4, 5, 7]
                                lgi_ = [0]

                                def gen_kt(h, c, gen=gen, Bgen=Bgen):
                                    for m in range(2):
                                        K.mm(gen, w_ukv[:, m, h * 128:(h + 1) * 128], kvnT[:, m, c * 512:(c + 1) * 512],
                                             m == 0, m == 1, [Bwukv, Bkvn[c]], [Bgen])
                                    K.cp(DVE, KT[64:128, c * 512:(c + 1) * 512], gen[64:128, :], [Bgen], [BKTn[c]])

                                def gen_kt2(h, c, gen=gen, Bgen=Bgen):
                                    wk = w_ukv[:, :, h * 128 + 64:h * 128 + 128]
                                    for m in range(2):
                                        for t_ in range(2):
                                            cc_ = c + t_
                                            K.mm(gen[64 * t_:64 * t_ + 64, :], wk[:, m, :], kvnT[:, m, cc_ * 512:(cc_ + 1) * 512],
                                                 m == 0, m == 1, [Bwukv, Bkvn[cc_]], [Bgen], tile_position=(0, 64 * t_))
                                    K.cp(DVE, KT[64:128, c * 512:(c + 1) * 512], gen[0:64, :], [Bgen], [BKTn[c]])
                                    K.cp(DVE, KT[64:128, (c + 1) * 512:(c + 2) * 512], gen[64:128, :], [Bgen], [BKTn[c + 1]])

                                def gen_v(h, g8, gen=gen, Bgen=Bgen):
                                    for j in range(8):
                                        kt = g8 * 8 + j
                                        for m in range(2):
                                            K.mm(gen[:, j * 64:(j + 1) * 64], kvnT[:, m, kt * 128:(kt + 1) * 128],
                                                 w_ukv[:, m, h * 128:h * 128 + 64], m == 0, m == 1,
                                                 [Bwukv, Bkvn[kt // 4]], [Bgen])
                                    K.cp(DVE, V[:, g8 * 8:(g8 + 1) * 8, 0:64],
                                         gen.rearrange("p (j d) -> p j d", j=8), [Bgen], [BV[g8]])

                                gb_ = [0, 1, 2, 3, 4, 5, 7]
                                gi_ = 0
                                for c in range(0, NCK, 2):
                                    b_ = gb_[gi_ % len(gb_)]
                                    gi_ += 1
                                    gen_kt2(0, c, bank[b_], Bbank[b_])
                                for g8 in range(NT // 8):
                                    b_ = gb_[gi_ % len(gb_)]
                                    gi_ += 1
                                    gen_v(0, g8, bank[b_], Bbank[b_])
                                NG = NT // 2
                                NQC = QB // 512
                                for h in range(8):
                                    groups = [(qc, g) for qc in range(NQC) for g in range(NG)]
                                    kt_done, v_done = [0], [0]

                                    lastgen = h + 1 < 8
                                    slots_ = [(idx_ % 2 if (lastgen and q_ == NQC - 1) else idx_ % NSC)
                                              for idx_, (q_, g_) in enumerate(groups)]
                                    nq = [0]

                                    def emit_qk(idx):
                                        qc, g = groups[idx]
                                        s_ = slots_[idx]
                                        for j in range(2):
                                            kt = 2 * g + j
                                            K.mm(sc[s_][:, j * 512:(j + 1) * 512], KT[:, kt * 128:(kt + 1) * 128],
                                                 QT[:, h, qc * 512:(qc + 1) * 512], True, True,
                                                 [BKTr[kt // 4], BKTn[kt // 4], BQT[qc][h]], [Bsc[s_][j]])

                                    def advance_qk(idx):
                                        while nq[0] < len(groups) and nq[0] <= idx + NSC - 1 and \
                                                all(slots_[k] != slots_[nq[0]] for k in range(idx, nq[0])):
                                            emit_qk(nq[0])
                                            nq[0] += 1

                                    for idx, (qc, g) in enumerate(groups):
                                        s_ = slots_[idx]
                                        acc, Bacc = bank[accb_[qc]], Bbank[accb_[qc]]
                                        advance_qk(idx)
                                        p_ = pi % 4
                                        pi += 1
                                        K.act(Pb[p_][:], sc[s_][:], AF.Exp, Bsc[s_], [BP[p_]])
                                        for j in range(2):
                                            kt = 2 * g + j
                                            K.mm(acc[0:65, :], V[:, kt, 0:65], Pb[p_][:, j * 512:(j + 1) * 512],
                                                 g == 0 and j == 0, g == NG - 1 and j == 1, [BV[kt // 8], BP[p_]],
                                                 [Bacc])
                                        if g == NG - 1:
                                            ob = (h % 2) * 64
                                            K.cp(DVE, osb[ob:ob + 64, :], acc[0:64, :], [Bacc], [Bosb])
                                            K.cp(DVE, Rs[64:65, :], acc[64:65, :], [Bacc], [BRs])
                                            K.recip(Rs[64:65, :], Rs[64:65, :], [BRs], [BRs])
                                            K.cp(DVE, Rhl[64:65, 0:512], Rs[64:65, :], [BRs], [BRr])
                                            K.tt(DVE, Rhl[64:65, 512:1024], Rs[64:65, :], Rhl[64:65, 0:512], ALU.subtract,
                                                 [BRs, BRr], [BRr])
                                            pending.append((h, blk * NQC + qc, fbb_[qc]))
                                        if pending and (g == min(6, NG - 2) or (h == 7 and qc == NQC - 1 and g == NG - 1)):
                                            fh, qcg, fb_ = pending.pop(0)
                                            ob = (fh % 2) * 64
                                            fbk, Bfbk = bank[fb_], Bbank[fb_]
                                            K.mm(fbk, C.ones[64:65, 0:128], Rhl[64:65, 0:512], True, False, [C.Bconst, BRr], [Bfbk])
                                            K.mm(fbk, C.ones[64:65, 0:128], Rhl[64:65, 512:1024], False, True, [C.Bconst, BRr], [Bfbk])
                                            K.tt(DVE, oT[ob:ob + 64, fh // 2, qcg * 512:(qcg + 1) * 512],
                                                 fbk[ob:ob + 64, :], osb[ob:ob + 64, :], ALU.mult, [Bfbk, Bosb], [BoT[qcg]])
                                        if h + 1 < 8 and qc == NQC - 1:
                                            ge = min(nq[0] - 1 - qc * NG, NG - 1)
                                            while kt_done[0] + 2 <= (ge + 1) // 2:
                                                b_ = lgb_[lgi_[0] % 3]
                                                lgi_[0] += 1
                                                gen_kt2(h + 1, kt_done[0], bank[b_], Bbank[b_])
                                                kt_done[0] += 2
                                            while v_done[0] < (g + 1) // 4:
                                                b_ = lgb_[lgi_[0] % 3]
                                                lgi_[0] += 1
                                                gen_v(h + 1, v_done[0], bank[b_], Bbank[b_])
                                                v_done[0] += 1
                                T.flush()

                with contextlib.ExitStack() as ph:
                    NCH = NQ // 512
                    w3 = sb(ph, "w3", [128, 8, 2560], BF16)
                    w_out = sb(ph, "w_out", [128, 8, D], BF16)
                    npost = sb(ph, "npost", [128, D], F32)
                    Bw3, Bwout, Bnpost = Buf("w3"), Buf("wout"), Buf("npost")
                    L = LNT(ph, 4)
                    xnT = [sb(ph, f"xnT{i}", [128, 8, 512], BF16) for i in range(3)]
                    BxnT = [Buf(f"xnT{i}") for i in range(3)]
                    xnTh = sb(ph, "xnTh", [128, 8, 128], BF16)
                    BxnTh = Buf("xnTh")
                    cu = sb(ph, "cu", [128, 514], F32)
                    hcT = sb(ph, "hcT", [128, 4, 16], F32)
                    cuh = sb(ph, "cuh", [128, 4, 16], F32)
                    Bcuh = Buf("cuh")
                    cT = sb(ph, "cT", [128, 512], F32)
                    sg = [sb(ph, f"sg{i}", [128, 512], F32) for i in range(2)]
                    szt = [sb(ph, f"szt{i}", [128, 512], F32) for i in range(2)]
                    bzt2 = [sb(ph, f"bzt{i}", [128, 512], F32) for i in range(2)]
                    Bbzt2 = [Buf("bzt0"), Buf("bzt1")]
                    a0 = [sb(ph, f"a0{i}", [128, 512], F32) for i in range(2)]
                    mix = [sb(ph, f"mix{i}", [128, 8, 512], BF16) for i in range(2)]
                    y1 = sb(ph, "y1", [128, D], F32)
                    yo = [sb(ph, f"yo{i}", [128, D], F32) for i in range(2)]
                    xres = [sb(ph, f"xres{i}", [128, D], F32) for i in range(2)]
                    Bcu, Bhc, BcT, Bbzt, By1 = (Buf(n) for n in ("cu", "hc", "cT", "bzt", "y1"))
                    Bsg = [Buf("sg0"), Buf("sg1")]
                    Bszt = [Buf("szt0"), Buf("szt1")]
                    Ba0 = [Buf("a00"), Buf("a01")]
                    Bmix = [[Buf(f"mix{i}_{k}") for k in range(8)] for i in range(2)]
                    Byo = [Buf("yo0"), Buf("yo1")]
                    Bxres = [Buf("xres0"), Buf("xres1")]
                    stg3 = [t_[:] for t_ in L.xt] + [t_[:] for t_ in xres] + [t_[:] for t_ in yo] + [y1[:]]
                    Bstg3 = L.Bxt + Bxres + Byo + [By1]
                    wctr = [0]
                    for cb in range(0, 2560, 1024):
                        ncol = min(1024, 2560 - cb)
                        K.load_fold(w3, Bw3, w3_d, 8, ncol, C.npre, stg3, Bstg3, col0=cb, ctr=wctr)
                    K.load_fold(w_out, Bwout, w_out_d, 8, D, None, stg3, Bstg3, ctr=wctr)
                    K.dma(SP, npost[:], npost_d.partition_broadcast(128), (), [Bnpost])
                    hr = job["halo"]
                    nh2 = 2 * NCH
                    hs = L.front(xh[hr:hr + nh2, :], nrows=nh2, pre_zero=True)
                    L.back(hs, xnTh, BxnTh, 0, bank[0], Bbank[0])
                    hv = bank[1].rearrange("p (g c) -> p g c", g=8)
                    for cc in range(4):
                        for ty, col0 in ((0, 0), (1, 1024)):
                            for k in range(8):
                                K.mm(hv[:, cc * 2 + ty, 0:nh2], w3[:, k, col0 + cc * 128:col0 + (cc + 1) * 128],
                                     xnTh[:, k, 0:nh2], k == 0, k == 7, [Bw3, BxnTh], [Bbank[1]])
                    hv4 = bank[1].rearrange("p (cc t c) -> p cc t c", cc=4, t=2)
                    K.cp(DVE, hcT[:, :, 0:nh2], hv4[:, :, 1, 0:nh2], [Bbank[1]], [Bhc])
                    K.tt(DVE, cuh[:, :, 0:nh2], hv4[:, :, 0, 0:nh2], hcT[:, :, 0:nh2], ALU.mult, [Bbank[1], Bhc], [Bcuh])
                    sgc = [0]
                    yctr = [0]
                    octr = [0]

                    def silu_gate(bk, Bbk):
                        i = sgc[0] % 2
                        sgc[0] += 1
                        K.act(sg[i][:], bk, AF.Exp, [Bbk], [Bsg[i]], scale=-1.0)
                        K.act(sg[i][:], sg[i][:], AF.Ln, [Bsg[i]], [Bsg[i]], bias=C.onesf[:, 0:1])
                        K.act(sg[i][:], sg[i][:], AF.Exp, [Bsg[i]], [Bsg[i]], scale=-1.0)
                        K.tt(DVE, szt[i][:], bk, sg[i][:], ALU.mult, [Bbk, Bsg[i]], [Bszt[i]])
                        return szt[i], Bszt[i]

                    lslots = {}

                    def st3_lnt_front(j):
                        lslots[j] = L.chunk_front(lambda t: job["xq"][j * 512 + t * 128:j * 512 + (t + 1) * 128, :])

                    def st3_lnt_back(j):
                        L.chunk_back(lslots.pop(j), xnT[j % 3], BxnT[j % 3], bank[0:2], Bbank[0:2])

                    def st3_lnt(j):
                        st3_lnt_front(j)
                        st3_lnt_back(j)

                    def st3_units(j):
                        X, BX = xnT[j % 3], BxnT[j % 3]
                        mx, Bmx = mix[j % 2], Bmix[j % 2]
                        if j > 0:
                            Lh, BLh, lc = xnT[(j - 1) % 3], BxnT[(j - 1) % 3], 511
                        else:
                            Lh, BLh, lc = xnTh, BxnTh, 0
                        if j + 1 < NCH:
                            Rh, BRh, rcol = xnT[(j + 1) % 3], BxnT[(j + 1) % 3], 0
                        else:
                            Rh, BRh, rcol = xnTh, BxnTh, 1

                        def mm_uc(cc):
                            for bi, col0 in ((2, 0), (3, 1024)):
                                for k in range(8):
                                    K.mm(bank[bi], w3[:, k, col0 + cc * 128:col0 + (cc + 1) * 128], X[:, k, :], k == 0,
                                         k == 7, [Bw3, BX], [Bbank[bi]])

                        def ew_uc(cc):
                            a, Ba = a0[cc % 2], Ba0[cc % 2]
                            K.cp(ACT, cT[:], bank[3], [Bbank[3]], [BcT])
                            K.tt(DVE, cu[:, 1:513], bank[2], cT[:], ALU.mult, [Bbank[2], BcT], [Bcu])
                            K.cp(DVE, cu[:, 0:1], cuh[:, cc, 2 * j:2 * j + 1], [Bcuh], [Bcu])
                            K.cp(DVE, cu[:, 513:514], cuh[:, cc, 2 * j + 1:2 * j + 2], [Bcuh], [Bcu])
                            K.ts(DVE, a[:], cu[:, 1:513], C.convw[:, cc * 3 + 1:cc * 3 + 2], ALU.mult, [Bcu, C.Bconst], [Ba])
                            K.stt(a[:], cu[:, 0:512], C.convw[:, cc * 3:cc * 3 + 1], a[:], ALU.mult, ALU.add,
                                  [Bcu, Ba, C.Bconst], [Ba])
                            K.stt(a[:], cu[:, 2:514], C.convw[:, cc * 3 + 2:cc * 3 + 3], a[:], ALU.mult, ALU.add,
                                  [Bcu, Ba, C.Bconst], [Ba])

                        def mm_bz(cc):
                            for bi, col0 in ((4, 512), (5, 1536)):
                                for k in range(8):
                                    K.mm(bank[bi], w3[:, k, col0 + cc * 128:col0 + (cc + 1) * 128], X[:, k, :], k == 0,
                                         k == 7, [Bw3, BX], [Bbank[bi]])

                        def ew_bz(cc):
                            a, Ba = a0[cc % 2], Ba0[cc % 2]
                            sz_, Bsz_ = silu_gate(bank[5], Bbank[5])
                            bzt, Bbzt_ = bzt2[cc % 2], Bbzt2[cc % 2]
                            K.tt(DVE, bzt[:], bank[4], sz_[:], ALU.mult, [Bbank[4], Bsz_], [Bbzt_])
                            K.tt(POOL, mx[:, cc, :], a[:], bzt[:], ALU.mult, [Ba, Bbzt_], [Bmx[cc]])

                        zab = (2, 4, 3, 5)

                        def mm_za(p):
                            bi = zab[p]
                            for k in range(8):
                                K.mm(bank[bi], w3[:, k, 2048 + p * 128:2048 + (p + 1) * 128], X[:, k, :], k == 0, k == 7,
                                     [Bw3, BX], [Bbank[bi]])

                        def ew_za(p):
                            bi = zab[p]
                            sz_, Bsz_ = silu_gate(bank[bi], Bbank[bi])
                            K.tt(POOL, mx[:, 4 + p, :], oT[:, p, j * 512:(j + 1) * 512], sz_[:], ALU.mult,
                                 [BoT[j], Bsz_], [Bmx[4 + p]])

                        units = []
                        for cc in range(4):
                            units.append((mm_uc, ew_uc, cc))
                            units.append((mm_bz, ew_bz, cc))
                        for p in range(4):
                            units.append((mm_za, ew_za, p))
                        units[0][0](units[0][2])
                        for ui, (mmf, ewf, arg) in enumerate(units):
                            if ui + 1 < len(units):
                                units[ui + 1][0](units[ui + 1][2])
                            ewf(arg)

                    obanks = (6, 7)

                    def st3_out(j):
                        mx, Bmx = mix[j % 2], Bmix[j % 2]
                        for t in range(4):
                            r0 = j * 512 + t * 128
                            xr, Bxr = xres[yctr[0] % 2], Bxres[yctr[0] % 2]
                            yb, Byb = yo[yctr[0] % 2], Byo[yctr[0] % 2]
                            i8 = yctr[0] % 4
                            yctr[0] += 1
                            K.dma(SP, xr[:], job["xq"][r0:r0 + 128, :], (), [Bxr])
                            ssa, ssb2 = C.ss[:, i8 * 2:i8 * 2 + 1], C.ss[:, i8 * 2 + 1:i8 * 2 + 2]
                            rsc = C.rs[:, i8 * 2:i8 * 2 + 1]
                            Bssa, Bssb2, Brsc = C.Bss[i8 * 2], C.Bss[i8 * 2 + 1], C.Brs[i8 * 2]
                            for nh in range(2):
                                bi = obanks[octr[0] % 2]
                                octr[0] += 1
                                for k in range(8):
                                    K.mm(bank[bi], mx[:, k, t * 128:(t + 1) * 128], w_out[:, k, nh * 512:(nh + 1) * 512],
                                         k == 0, k == 7, [Bmx[k], Bwout], [Bbank[bi]])
                                K.act(yb[:, nh * 512:(nh + 1) * 512], bank[bi], AF.Square, [Bbank[bi]],
                                      [Byb, Bssa if nh == 0 else Bssb2], accum=ssa if nh == 0 else ssb2)
                                K.tt(DVE, y1[:, nh * 512:(nh + 1) * 512], bank[bi], npost[:, nh * 512:(nh + 1) * 512],
                                     ALU.mult, [Bbank[bi], Bnpost], [By1])
                            K.tt(DVE, ssa, ssa, ssb2, ALU.add, [Bssa, Bssb2], [Bssa])
                            K.act(rsc, ssa, AF.Ln, [Bssa], [Brsc], scale=1.0 / D, bias=C.epsb[:, 0:1])
                            K.act(rsc, rsc, AF.Exp, [Brsc], [Brsc], scale=-0.5)
                            K.stt(yb[:], y1[:], rsc, xr[:], ALU.mult, ALU.add, [By1, Brsc, Bxr], [Byb])
                            K.dma(POOL, job["y"][r0:r0 + 128, :], yb[:], [Byb], ())

                    st3_lnt(0)
                    if NCH > 1:
                        st3_lnt(1)
                    for j in range(NCH + 1):
                        if j + 2 < NCH:
                            st3_lnt_front(j + 2)
                        if j - 1 >= 0:
                            st3_out(j - 1)
                        if j < NCH:
                            st3_units(j)
                        if j + 2 < NCH:
                            st3_lnt_back(j + 2)
                    T.flush(final=(job is jobs[-1]))
    return nc


def _rope_tables(S):
    freqs = (1.0 / (np.float32(10000.0) ** (np.arange(0, 32, 2, dtype=np.float32) / np.float32(32)))).astype(np.float32)
    ang = (np.arange(S, dtype=np.float32)[:, None] * freqs[None, :]).astype(np.float32)
    cos = np.cos(ang).astype(np.float32).T
    sin = np.sin(ang).astype(np.float32).T
    tab = np.empty((2, 32, S), np.float32)
    tab[0, 0:16] = cos
    tab[0, 16:32] = cos
    tab[1, 0:16] = -sin
    tab[1, 16:32] = sin
    return tab


_PROGRAM = None


def kernel(x_prompt, x_sample, norm_pre, w_in, conv_w, q_norm, w_uq, kv_norm, w_ukv, w_out, norm_post):
    global _PROGRAM
    f = np.float32
    x_prompt = np.asarray(x_prompt, f)
    x_sample = np.asarray(x_sample, f)
    w_in = np.asarray(w_in, f)[0]
    w_uq = np.asarray(w_uq, f)[0]
    w_ukv = np.asarray(w_ukv, f)[0]
    w_out_ = np.asarray(w_out, f)[0]
    norm_pre = np.asarray(norm_pre, f)[0]
    q_norm = np.asarray(q_norm, f)[0]
    kv_norm = np.asarray(kv_norm, f)[0]
    norm_post = np.asarray(norm_post, f)[0]
    conv_w = np.asarray(conv_w, f)[0]

    kpe = w_in[:, 2688:2720]
    w_kvp = np.concatenate([w_in[:, 2432:2688], kpe, kpe[:, 16:32], kpe[:, 0:16]], axis=1)
    w_ql = w_in[:, 2048:2432]
    uq = w_uq.reshape(384, 8, 96)
    rope_c = uq[:, :, 64:96]
    swap_c = np.concatenate([uq[:, :, 80:96], uq[:, :, 64:80]], axis=2)
    nope_c = uq[:, :, 0:64]
    w_uqp = np.concatenate([rope_c[:, 0:4].reshape(384, 128), rope_c[:, 4:8].reshape(384, 128),
                            swap_c[:, 0:4].reshape(384, 128), swap_c[:, 4:8].reshape(384, 128),
                            nope_c.reshape(384, 512)], axis=1)
    ukv = w_ukv.reshape(256, 8, 128)
    w_ukvp = np.concatenate([ukv[:, :, 64:128], ukv[:, :, 0:64]], axis=2).reshape(256, 1024)
    w3 = np.concatenate([w_in[:, 0:2048], w_in[:, 2720:3232]], axis=1)
    npre = norm_pre.reshape(8, 128).T
    qnv = q_norm.reshape(3, 128).T
    kvv = kv_norm.reshape(2, 128).T
    convw = conv_w.reshape(3, 4, 128).transpose(2, 1, 0).reshape(128, 12)
    rope = _rope_tables(S_S)
    shared = dict(
        rope=rope, ropep4=np.tile(rope[:, :, 0:S_P], (1, 4, 1)), w_kvp=w_kvp, w_ql=w_ql, w_uqp=w_uqp, w_ukvp=w_ukvp, w3=w3, w_out=w_out_, npre=npre, qnv=qnv,
        kvv=kvv, convw=convw, npost=norm_post.reshape(1, D), ident=np.eye(128, dtype=f))
    shared = {k: np.ascontiguousarray(v, dtype=f) for k, v in shared.items()}
    in_maps = []
    for c in range(NCORES):
        sidx, q0 = c // 4, (c % 4) * NQ_S
        xs = x_sample[sidx]
        xh = np.zeros((24, D), f)
        xpc = x_prompt[c]
        for j in range(S_P // 512):
            if j * 512 - 1 >= 0:
                xh[2 * j] = xpc[j * 512 - 1]
            if (j + 1) * 512 < S_P:
                xh[2 * j + 1] = xpc[(j + 1) * 512]
        for j in range(NQ_S // 512):
            lo_, hi_ = q0 + j * 512 - 1, q0 + (j + 1) * 512
            if lo_ >= 0:
                xh[8 + 2 * j] = xs[lo_]
            if hi_ < S_S:
                xh[8 + 2 * j + 1] = xs[hi_]
        m = dict(shared)
        m["xp"] = np.ascontiguousarray(x_prompt[c])
        m["xs"] = np.ascontiguousarray(xs)
        m["xq"] = np.ascontiguousarray(xs[q0:q0 + NQ_S])
        m["xh"] = xh
        m["ropeq"] = np.ascontiguousarray(np.tile(rope[:, :, q0:q0 + NQ_S], (1, 4, 1)))
        in_maps.append(m)
    if _PROGRAM is None:
        _PROGRAM = build_program()
    res = run_bass_kernel_spmd(_PROGRAM, in_maps, core_ids=list(range(NCORES)))
    y_prompt = np.stack([np.asarray(res.results[c]["yp"], f) for c in range(NCORES)], axis=0)
    y_sample = np.empty((2, S_S, D), f)
    for c in range(NCORES):
        sidx, q0 = c // 4, (c % 4) * NQ_S
        y_sample[sidx, q0:q0 + NQ_S] = np.asarray(res.results[c]["ys"], f)
    return (y_prompt, y_sample)
```

```python
import contextlib
import math
import numpy as np
import concourse.bass as bass
import concourse.mybir as mybir
from concourse.bass_utils import run_bass_kernel_spmd

F32 = mybir.dt.float32
BF16 = mybir.dt.bfloat16
ALU = mybir.AluOpType
AF = mybir.ActivationFunctionType

PE, ACT, DVE, POOL, SP = "pe", "act", "dve", "pool", "sp"
COMPUTE = (PE, ACT, DVE, POOL)
EPS = 1e-6
D = 1024
NCORES = 8
S_P, S_S, NQ_S = 2048, 16384, 4096
QB = 2048


class Buf:
    __slots__ = ("name", "excl", "lw", "rd")

    def __init__(self, name, excl=False):
        self.name = name
        self.excl = excl
        self.lw = None
        self.rd = {}


class Tracker:
    DMA_RING = 8

    def __init__(self, nc):
        self.nc = nc
        self.ops = []
        self.sems = {e: nc.alloc_semaphore(name=f"s_{e}") for e in COMPUTE}
        self.dma_sems = {q: [nc.alloc_semaphore(name=f"d_{q}_{i}") for i in range(self.DMA_RING)]
                         for q in (SP, POOL)}
        self.sig_cnt = {e: 0 for e in COMPUTE}
        self.dma_cnt = {q: 0 for q in (SP, POOL)}
        self.waited = {e: {p: 0 for p in COMPUTE} for e in (PE, ACT, DVE, POOL, SP)}
        self.dma_waited = {e: set() for e in (PE, ACT, DVE, POOL, SP)}
        self.flushed = 0
        self.last_dma = {}
        self.bar_cnt = None
        self.bar_dma = None

    def _rec(self, eng, fn, reads, writes, dma=False):
        i = len(self.ops)
        deps = set()
        for b in reads:
            if b.lw is not None:
                deps.add(b.lw)
            if b.excl:
                deps.update(b.rd.values())
        for b in writes:
            if b.lw is not None:
                deps.add(b.lw)
            deps.update(b.rd.values())
        key = eng if not dma else ("dma", i)
        for b in reads:
            if b.excl:
                b.lw = i
                b.rd = {}
            else:
                b.rd[key] = i
        for b in writes:
            b.lw = i
            b.rd = {}
        deps.discard(i)
        self.ops.append(dict(eng=eng, fn=fn, deps=deps, dma=dma, need=False))
        return i

    def op(self, eng, fn, reads=(), writes=()):
        return self._rec(eng, fn, reads, writes)

    def dma(self, q, fn, reads=(), writes=()):
        return self._rec(q, fn, reads, writes, dma=True)

    def flush(self, final=False):
        nc = self.nc
        ops = self.ops
        lo, hi = self.flushed, len(ops)
        for i in range(lo, hi):
            o = ops[i]
            for d in o["deps"]:
                p = ops[d]
                if p["dma"]:
                    continue
                if p["eng"] != o["eng"] or o["dma"] or p["eng"] != PE:
                    p["need"] = True
        _per = {e: [] for e in COMPUTE}
        for i in range(lo, hi):
            if not ops[i]["dma"]:
                _per[ops[i]["eng"]].append(i)
        for e in COMPUTE:
            if _per[e]:
                ops[_per[e][-1]]["need"] = True
        for i in range(lo, hi):
            o = ops[i]
            if o["dma"]:
                q = o["eng"]
                n = self.dma_cnt[q]
                self.dma_cnt[q] += 1
                o["ring"] = n % self.DMA_RING
                o["val"] = 16 * (n // self.DMA_RING + 1)
            elif o["need"]:
                self.sig_cnt[o["eng"]] += 1
                o["cnt"] = self.sig_cnt[o["eng"]]
        per = {e: [] for e in (PE, ACT, DVE, POOL, SP)}
        for i in range(lo, hi):
            per[ops[i]["eng"]].append(i)
        bar_cnt, bar_dma = self.bar_cnt, self.bar_dma

        def run(ename, eobj):
            waited = self.waited[ename]
            dwaited = self.dma_waited[ename]
            if bar_cnt is not None and per[ename]:
                for p_, c_ in bar_cnt.items():
                    if c_ > 0 and waited[p_] < c_:
                        eobj.wait_ge(self.sems[p_], c_)
                        waited[p_] = c_
                for (q_, r_), v_ in sorted(bar_dma.items()):
                    eobj.wait_ge(self.dma_sems[q_][r_], v_)
            for i in per[ename]:
                o = ops[i]
                for d in sorted(o["deps"]):
                    p = ops[d]
                    if p["dma"]:
                        if d not in dwaited:
                            eobj.wait_ge(self.dma_sems[p["eng"]][p["ring"]], p["val"])
                            dwaited.add(d)
                    else:
                        if p["eng"] == ename and ename == PE and not o["dma"]:
                            continue
                        c = p.get("cnt")
                        if c is None:
                            continue
                        if waited[p["eng"]] < c:
                            eobj.wait_ge(self.sems[p["eng"]], c)
                            waited[p["eng"]] = c
                if o["dma"]:
                    sem = self.dma_sems[ename][o["ring"]]
                    if o["val"] > 16:
                        eobj.wait_ge(sem, o["val"] - 16)
                    ins = o["fn"](eobj)
                    ins.then_inc(sem, 16)
                    self.last_dma[(ename, o["ring"])] = o["val"]
                else:
                    ins = o["fn"](eobj)
                    if o["need"]:
                        ins.then_inc(self.sems[ename], 1)
                o["fn"] = None
            if final and ename == SP:
                for (q, r), v in sorted(self.last_dma.items()):
                    eobj.wait_ge(self.dma_sems[q][r], v)

        with nc.Block() as block:
            block.tensor(lambda e: run(PE, e))
            block.scalar(lambda e: run(ACT, e))
            block.vector(lambda e: run(DVE, e))
            block.gpsimd(lambda e: run(POOL, e))
            block.sync(lambda e: run(SP, e))
        self.flushed = hi
        self.bar_cnt = dict(self.sig_cnt)
        self.bar_dma = dict(self.last_dma)


class KB:
    def __init__(self, nc):
        self.nc = nc
        self.T = Tracker(nc)

    def mm(self, out, lhsT, rhs, start, stop, reads, writes, tile_position=None):
        if tile_position is None:
            self.T.op(PE, lambda e: e.matmul(out, lhsT=lhsT, rhs=rhs, start=start, stop=stop), reads, writes)
        else:
            self.T.op(PE, lambda e: e.matmul(out, lhsT=lhsT, rhs=rhs, start=start, stop=stop,
                                             tile_position=tile_position), reads, writes)

    def tr(self, out, in_, ident, reads, writes):
        self.T.op(PE, lambda e: e.transpose(out, in_, ident), reads, writes)

    def act(self, out, in_, func, reads, writes, scale=1.0, bias=None, accum=None):
        def f(e):
            kw = dict(out=out, in_=in_, func=func, scale=scale)
            if bias is not None:
                kw["bias"] = bias
            if accum is not None:
                kw["accum_out"] = accum
            return e.activation(**kw)
        self.T.op(ACT, f, reads, writes)

    def tt(self, eng, out, in0, in1, op, reads, writes):
        self.T.op(eng, lambda e: e.tensor_tensor(out=out, in0=in0, in1=in1, op=op), reads, writes)

    def ts(self, eng, out, in0, s1, op0, reads, writes):
        self.T.op(eng, lambda e: e.tensor_scalar(out=out, in0=in0, scalar1=s1, scalar2=None, op0=op0), reads, writes)

    def stt(self, out, in0, scalar, in1, op0, op1, reads, writes):
        self.T.op(DVE, lambda e: e.scalar_tensor_tensor(out=out, in0=in0, scalar=scalar, in1=in1, op0=op0, op1=op1),
                  reads, writes)

    def cp(self, eng, out, in_, reads, writes):
        if eng == ACT:
            self.act(out, in_, AF.Copy, reads, writes)
        else:
            self.T.op(eng, lambda e: e.tensor_copy(out=out, in_=in_), reads, writes)

    def recip(self, out, in_, reads, writes):
        self.T.op(DVE, lambda e: e.reciprocal(out=out, in_=in_), reads, writes)

    def memset(self, eng, ap, val, writes):
        self.T.op(eng, lambda e: e.memset(ap, val), (), writes)

    def dma(self, q, out, in_, reads, writes):
        self.T.dma(q, lambda e: e.dma_start(out=out, in_=in_), reads, writes)

    def load_fold(self, dst, Bdst, src, nk, ncols, scale_ap, stg, Bstg, col0=0, ctr=None):
        ctr = ctr if ctr is not None else [0]
        for k in range(nk):
            i = ctr[0]
            ctr[0] += 1
            s = i % len(stg)
            st_ap = stg[s][:, 0:ncols]
            self.dma(SP, st_ap, src[k * 128:(k + 1) * 128, col0:col0 + ncols], (), [Bstg[s]])
            d_ap = dst[:, k, col0:col0 + ncols]
            if i % 2 == 0:
                if scale_ap is None:
                    self.cp(DVE, d_ap, st_ap, [Bstg[s]], [Bdst])
                else:
                    self.ts(DVE, d_ap, st_ap, scale_ap[:, k:k + 1], ALU.mult, [Bstg[s]], [Bdst])
            else:
                if scale_ap is None:
                    self.act(d_ap, st_ap, AF.Copy, [Bstg[s]], [Bdst])
                else:
                    self.act(d_ap, st_ap, AF.Copy, [Bstg[s]], [Bdst], scale=scale_ap[:, k:k + 1])

    def lnt_tile(self, C, x_rows, xt, Bxt, xn, Bxn, xnT, BxnT, tcol, trb, Btr, evac, i, nrows=128):
        ssc = C.ss[:, (i % 8):(i % 8) + 1]
        rsc = C.rs[:, (i % 8):(i % 8) + 1]
        Bss, Brs = C.Bss[i % 8], C.Brs[i % 8]
        self.dma(SP, xt[0:nrows, :], x_rows, (), [Bxt])
        self.act(xn, xt, AF.Square, [Bxt], [Bxn, Bss], accum=ssc)
        self.act(rsc, ssc, AF.Ln, [Bss], [Brs], scale=1.0 / D, bias=C.epsb[:, 0:1])
        self.act(rsc, rsc, AF.Exp, [Brs], [Brs], scale=-0.5)
        self.ts(DVE, xn, xt, rsc, ALU.mult, [Bxt, Brs], [Bxn])
        trv = trb.bitcast(BF16)
        for k in range(8):
            self.tr(trv[:, k * 128:(k + 1) * 128], xn[:, k * 128:(k + 1) * 128], C.ident[:], [Bxn, C.Bconst], [Btr])
        self.cp(evac, xnT[:, :, tcol * 128:(tcol + 1) * 128], trv.rearrange("p (k t) -> p k t", k=8), [Btr], [BxnT])

    def rstd_bcast(self, C, banks, Bbanks, nm, sq, Bsq, ssb, Bssb, R, BR, dim):
        for m in range(nm):
            self.act(sq[:, m, :], banks[m], AF.Square, [Bbanks[m]], [Bsq])
        for m in range(nm):
            self.mm(ssb, C.ones[:], sq[:, m, :], m == 0, m == nm - 1, [Bsq, C.Bconst], [Bssb])
        self.act(R[:], ssb, AF.Ln, [Bssb], [BR], scale=1.0 / dim, bias=C.epsb[:, 0:1])
        self.act(R[:], R[:], AF.Exp, [BR], [BR], scale=-0.5)


def pipeline(n, stages):
    ns = len(stages)
    for it in range(n + ns - 1):
        for si in range(ns - 1, -1, -1):
            i = it - si
            if 0 <= i < n:
                stages[si](i)


def build_program():
    nc = bass.Bass("TRN2", target_bir_lowering=False)
    K = KB(nc)
    T = K.T

    def din(name, shape):
        return nc.dram_tensor(name, list(shape), F32, kind="ExternalInput").ap()

    xp = din("xp", [S_P, D])
    xs = din("xs", [S_S, D])
    xq = din("xq", [NQ_S, D])
    xh = din("xh", [24, D])
    rope = din("rope", [2, 32, S_S])
    ropeq = din("ropeq", [2, 128, NQ_S])
    ropep4 = din("ropep4", [2, 128, S_P])
    w_kvp_d = din("w_kvp", [D, 320])
    w_ql_d = din("w_ql", [D, 384])
    w_uqp_d = din("w_uqp", [384, 1024])
    w_ukvp_d = din("w_ukvp", [256, 1024])
    w3_d = din("w3", [D, 2560])
    w_out_d = din("w_out", [D, D])
    npre_d = din("npre", [128, 8])
    qnv_d = din("qnv", [128, 3])
    kvv_d = din("kvv", [128, 2])
    convw_d = din("convw", [128, 12])
    npost_d = din("npost", [1, D])
    ident_d = din("ident", [128, 128])
    yp = nc.dram_tensor("yp", [S_P, D], F32, kind="ExternalOutput").ap()
    ys = nc.dram_tensor("ys", [NQ_S, D], F32, kind="ExternalOutput").ap()

    es = contextlib.ExitStack()
    uid = [0]

    def sb(stack, name, shape, dt):
        uid[0] += 1
        return stack.enter_context(nc.sbuf_tensor(f"{name}_u{uid[0]}", list(shape), dt))

    class C:
        pass

    class LNT:
        def __init__(self, ph, nslots):
            self.n = nslots
            self.xt = [sb(ph, "xt", [128, D], F32) for _ in range(nslots)]
            self.xn = [sb(ph, "xn", [128, D], BF16) for _ in range(nslots)]
            self.Bxt = [Buf("xt") for _ in range(nslots)]
            self.Bxn = [Buf("xn") for _ in range(nslots)]
            self.ctr = 0

        def front(self, rows, nrows=128, pre_zero=False):
            i = self.ctr
            self.ctr += 1
            s = i % self.n
            xt, xn, Bxt, Bxn = self.xt[s], self.xn[s], self.Bxt[s], self.Bxn[s]
            c8 = i % 8
            ssc, rsc = C.ss[:, c8:c8 + 1], C.rs[:, c8:c8 + 1]
            if pre_zero:
                K.memset(DVE, xt[:], 0.0, [Bxt])
            K.dma(SP, xt[0:nrows, :], rows, (), [Bxt])
            K.act(xn[:], xt[:], AF.Square, [Bxt], [Bxn, C.Bss[c8]], accum=ssc)
            K.act(rsc, ssc, AF.Ln, [C.Bss[c8]], [C.Brs[c8]], scale=1.0 / D, bias=C.epsb[:, 0:1])
            K.act(rsc, rsc, AF.Exp, [C.Brs[c8]], [C.Brs[c8]], scale=-0.5)
            K.ts(DVE, xn[:], xt[:], rsc, ALU.mult, [Bxt, C.Brs[c8]], [Bxn])
            return s

        def back(self, s, xnT, BxnT, tcol, trb, Btr):
            xn, Bxn = self.xn[s], self.Bxn[s]
            trv = trb.bitcast(BF16)
            for k in range(8):
                K.tr(trv[:, k * 128:(k + 1) * 128], xn[:, k * 128:(k + 1) * 128], C.ident[:], [Bxn, C.Bconst], [Btr])
            K.cp(DVE, xnT[:, :, tcol * 128:(tcol + 1) * 128], trv.rearrange("p (k t) -> p k t", k=8), [Btr], [BxnT])

        def chunk_front(self, row_fn):
            return [self.front(row_fn(t)) for t in range(4)]

        def chunk_back(self, slots, xnT, BxnT, trbs, Btrs):
            for t in range(4):
                self.back(slots[t], xnT, BxnT, t, trbs[t % len(trbs)], Btrs[t % len(trbs)])

        def chunk(self, row_fn, xnT, BxnT, trbs, Btrs):
            prev = None
            for t in range(4):
                s = self.front(row_fn(t))
                if prev is not None:
                    self.back(prev, xnT, BxnT, t - 1, trbs[(t - 1) % len(trbs)], Btrs[(t - 1) % len(trbs)])
                prev = s
            self.back(prev, xnT, BxnT, 3, trbs[3 % len(trbs)], Btrs[3 % len(trbs)])

    def rstd_tail(ssb, Bssb, R, BR, dim):
        K.act(R[:], ssb, AF.Ln, [Bssb], [BR], scale=1.0 / dim, bias=C.epsb[:, 0:1])
        K.act(R[:], R[:], AF.Exp, [BR], [BR], scale=-0.5)

    with es:
        pp = [es.enter_context(nc.psum_tensor(f"pp{i}", [128, 1024], F32)) for i in range(4)]
        bank = [pp[i // 2][:, (i % 2) * 512:(i % 2) * 512 + 512] for i in range(8)]
        Bbank = [Buf(f"bank{i}", excl=True) for i in range(8)]

        C.ident = sb(es, "ident", [128, 128], BF16)
        C.ones = sb(es, "ones", [128, 128], BF16)
        C.onesf = sb(es, "onesf", [128, 128], F32)
        C.epsb = sb(es, "epsb", [128, 1], F32)
        C.npre = sb(es, "npre", [128, 8], F32)
        C.qnv = sb(es, "qnv", [128, 3], F32)
        C.kvv = sb(es, "kvv", [128, 2], F32)
        C.convw = sb(es, "convw", [128, 12], F32)
        C.ss = sb(es, "ss", [128, 8], F32)
        C.rs = sb(es, "rs", [128, 8], F32)
        C.Bconst = Buf("const")
        C.Bss = [Buf(f"ss{i}") for i in range(8)]
        C.Brs = [Buf(f"rs{i}") for i in range(8)]

        with contextlib.ExitStack() as ph:
            idf = sb(ph, "idf", [128, 128], F32)
            Bidf = Buf("idf")
            K.dma(SP, idf[:], ident_d[:, :], (), [Bidf])
            K.dma(SP, C.npre[:], npre_d[:, :], (), [C.Bconst])
            K.dma(SP, C.qnv[:], qnv_d[:, :], (), [C.Bconst])
            K.dma(SP, C.kvv[:], kvv_d[:, :], (), [C.Bconst])
            K.dma(SP, C.convw[:], convw_d[:, :], (), [C.Bconst])
            K.cp(DVE, C.ident[:], idf[:], [Bidf], [C.Bconst])
            K.memset(DVE, C.ones[:], 1.0, [C.Bconst])
            K.memset(DVE, C.onesf[:], 1.0, [C.Bconst])
            K.memset(DVE, C.epsb[:], EPS, [C.Bconst])
            K.ts(DVE, C.qnv[:], C.qnv[:], 1.0 / math.sqrt(96.0), ALU.mult, [C.Bconst], [C.Bconst])
            T.flush()

        jobs = [
            dict(name="p", x=xp, xq=xp, S=S_P, NQ=S_P, rope=rope, ropeq=ropep4, halo=0, y=yp),
            dict(name="s", x=xs, xq=xq, S=S_S, NQ=NQ_S, rope=rope, ropeq=ropeq, halo=8, y=ys),
        ]
        for job in jobs:
            S, NQ = job["S"], job["NQ"]
            jn = job["name"]
            NCK = S // 512
            NT = S // 128
            with contextlib.ExitStack() as js:
                oT = sb(js, f"oT_{jn}", [128, 4, NQ], BF16)
                BoT = [Buf(f"oT{c}") for c in range(NQ // 512)]
                with contextlib.ExitStack() as ks:
                    kvnT = sb(ks, f"kvnT_{jn}", [128, 2, S], BF16)
                    KT = sb(ks, f"KT_{jn}", [128, S], BF16)
                    Bkvn = [Buf(f"kvn{c}") for c in range(NCK)]
                    BKTr = [Buf(f"ktr{c}") for c in range(NCK)]
                    BKTn = [Buf(f"ktn{c}") for c in range(NCK)]
                    Bwukv = Buf("wukv")

                    with contextlib.ExitStack() as ph:
                        w_kvp = sb(ph, "w_kvp", [128, 8, 320], BF16)
                        Bwkvp = Buf("wkvp")
                        L = LNT(ph, 3)
                        xnT = [sb(ph, f"xnT{i}", [128, 8, 512], BF16) for i in range(2)]
                        BxnT = [Buf(f"xnT{i}") for i in range(2)]
                        rc = [sb(ph, f"rc{i}", [32, 2, 512], F32) for i in range(2)]
                        Brc = [Buf(f"rc{i}") for i in range(2)]
                        sq = sb(ph, "sq", [128, 2, 512], BF16)
                        Bsq = Buf("sq")
                        R = sb(ph, "R", [128, 512], F32)
                        BR = Buf("R")
                        t1 = sb(ph, "t1", [32, 512], F32)
                        t2 = sb(ph, "t2", [32, 512], F32)
                        Bt1, Bt2 = Buf("t1"), Buf("t2")
                        K.load_fold(w_kvp, Bwkvp, w_kvp_d, 8, 320, C.npre, [t_[:] for t_ in L.xt] + [R[:]], L.Bxt + [BR])

                        def kv_lnt(c):
                            L.chunk(lambda t: job["x"][c * 512 + t * 128:c * 512 + (t + 1) * 128, :], xnT[c % 2],
                                    BxnT[c % 2], bank[0:2], Bbank[0:2])

                        def kv_s3(c):
                            s = c % 2
                            K.dma(SP, rc[s][:, 0, :], job["rope"][0, :, c * 512:(c + 1) * 512], (), [Brc[s]])
                            K.dma(SP, rc[s][:, 1, :], job["rope"][1, :, c * 512:(c + 1) * 512], (), [Brc[s]])
                            for m in range(2):
                                for k in range(8):
                                    K.mm(bank[2 + 2 * s + m], w_kvp[:, k, m * 128:(m + 1) * 128], xnT[s][:, k, :], k == 0,
                                         k == 7, [Bwkvp, BxnT[s]], [Bbank[2 + 2 * s + m]])
                            for k in range(8):
                                K.mm(bank[6][0:64, :], w_kvp[:, k, 256:320], xnT[s][:, k, :], k == 0, k == 7,
                                     [Bwkvp, BxnT[s]], [Bbank[6]])

                        def kv_s4a(c):
                            s = c % 2
                            cs = slice(c * 512, (c + 1) * 512)
                            for m in range(2):
                                K.act(sq[:, m, :], bank[2 + 2 * s + m], AF.Square, [Bbank[2 + 2 * s + m]], [Bsq])
                            K.tt(DVE, t1[:], bank[6][0:32, :], rc[s][:, 0, :], ALU.mult, [Bbank[6], Brc[s]], [Bt1])
                            K.tt(DVE, t2[:], bank[6][32:64, :], rc[s][:, 1, :], ALU.mult, [Bbank[6], Brc[s]], [Bt2])
                            K.tt(DVE, KT[0:32, cs], t1[:], t2[:], ALU.add, [Bt1, Bt2], [BKTr[c]])
                            K.memset(POOL, KT[32:64, cs], 0.0, [BKTr[c]])

                        def kv_s4b(c):
                            s = c % 2
                            cs = slice(c * 512, (c + 1) * 512)
                            for m in range(2):
                                K.mm(bank[7], C.ones[:], sq[:, m, :], m == 0, m == 1, [Bsq, C.Bconst], [Bbank[7]])
                            rstd_tail(bank[7], Bbank[7], R, BR, 256.0)
                            for m in range(2):
                                K.tt(DVE, kvnT[:, m, cs], bank[2 + 2 * s + m], R[:], ALU.mult,
                                     [Bbank[2 + 2 * s + m], BR], [Bkvn[c]])

                        pipeline(NCK, [kv_lnt, kv_s3, kv_s4a, kv_s4b])
                        T.flush()

                    for blk in range(NQ // QB):
                        with contextlib.ExitStack() as bs:
                            QT = sb(bs, f"QT_{jn}{blk}", [128, 8, QB], BF16)
                            BQT = [[Buf(f"QT{c}_{h_}") for h_ in range(8)] for c in range(QB // 512)]
                            with contextlib.ExitStack() as ph:
                                w_ql = sb(ph, "w_ql", [128, 8, 384], BF16)
                                w_uqp = sb(ph, "w_uqp", [128, 3, 1024], BF16)
                                Bwql, Bwuqp = Buf("wql"), Buf("wuqp")
                                L = LNT(ph, 2)
                                xnT0 = sb(ph, "xnT0", [128, 8, 512], BF16)
                                BxnT0 = Buf("xnT0")
                                rc0 = sb(ph, "rc0", [128, 2, 512], F32)
                                Brc0 = Buf("rc0")
                                sq = sb(ph, "sq", [128, 3, 512], BF16)
                                Bsq = Buf("sq")
                                qnT, BqnT = sq, Bsq
                                R = sb(ph, "R", [128, 512], F32)
                                BR = Buf("R")
                                t1 = sb(ph, "t1", [128, 512], F32)
                                t2 = sb(ph, "t2", [128, 512], F32)
                                Bt1, Bt2 = Buf("t1"), Buf("t2")
                                K.load_fold(w_ql, Bwql, w_ql_d, 8, 384, C.npre, [t_[:] for t_ in L.xt] + [R[:]], L.Bxt + [BR])
                                K.load_fold(w_uqp, Bwuqp, w_uqp_d, 3, 1024, C.qnv, [t_[:] for t_ in L.xt], L.Bxt)

                                def q_lnt(c):
                                    q0 = blk * QB + c * 512
                                    L.chunk(lambda t: job["xq"][q0 + t * 128:q0 + (t + 1) * 128, :], xnT0, BxnT0,
                                            bank[0:2], Bbank[0:2])

                                def q_s3(c):
                                    for m in range(3):
                                        for k in range(8):
                                            K.mm(bank[2 + m], w_ql[:, k, m * 128:(m + 1) * 128], xnT0[:, k, :], k == 0,
                                                 k == 7, [Bwql, BxnT0], [Bbank[2 + m]])

                                def q_s4(c):
                                    for m in range(3):
                                        K.act(sq[:, m, :], bank[2 + m], AF.Square, [Bbank[2 + m]], [Bsq])
                                    for m in range(3):
                                        K.mm(bank[5], C.ones[:], sq[:, m, :], m == 0, m == 2, [Bsq, C.Bconst], [Bbank[5]])
                                    rstd_tail(bank[5], Bbank[5], R, BR, 384.0)
                                    for m in range(3):
                                        K.tt(DVE, qnT[:, m, :], bank[2 + m], R[:], ALU.mult, [Bbank[2 + m], BR], [BqnT])

                                def q_s5(c):
                                    q0 = blk * QB + c * 512
                                    cs = slice(c * 512, (c + 1) * 512)
                                    K.memset(DVE, QT[32:64, :, cs], 0.0, BQT[c])
                                    K.dma(SP, rc0[:, 0, :], job["ropeq"][0, :, q0:q0 + 512], (), [Brc0])
                                    K.dma(SP, rc0[:, 1, :], job["ropeq"][1, :, q0:q0 + 512], (), [Brc0])
                                    for g in range(2):
                                        for bi_, blk_ in ((6, g), (7, 2 + g)):
                                            for m in range(3):
                                                K.mm(bank[bi_], w_uqp[:, m, blk_ * 128:(blk_ + 1) * 128], qnT[:, m, :],
                                                     m == 0, m == 2, [Bwuqp, BqnT], [Bbank[bi_]])
                                        K.tt(DVE, t1[:], bank[6], rc0[:, 0, :], ALU.mult, [Bbank[6], Brc0], [Bt1])
                                        K.tt(DVE, t2[:], bank[7], rc0[:, 1, :], ALU.mult, [Bbank[7], Brc0], [Bt2])
                                        for i in range(4):
                                            h = 4 * g + i
                                            K.tt(DVE, QT[0:32, h, cs], t1[32 * i:32 * i + 32, :], t2[32 * i:32 * i + 32, :],
                                                 ALU.add, [Bt1, Bt2], [BQT[c][h]])
                                    for p in range(4):
                                        bi_ = 6 + p % 2
                                        for m in range(3):
                                            K.mm(bank[bi_], w_uqp[:, m, (4 + p) * 128:(5 + p) * 128], qnT[:, m, :], m == 0, m == 2,
                                                 [Bwuqp, BqnT], [Bbank[bi_]])
                                        K.cp(ACT, QT[64:128, 2 * p, cs], bank[bi_][0:64, :], [Bbank[bi_]], [BQT[c][2 * p]])
                                        K.cp(ACT, QT[64:128, 2 * p + 1, cs], bank[bi_][64:128, :], [Bbank[bi_]],
                                             [BQT[c][2 * p + 1]])

                                pipeline(QB // 512, [q_lnt, q_s3, q_s4, q_s5])
                                T.flush()

                            with contextlib.ExitStack() as ph:
                                V = sb(ph, "V", [128, NT, 65], BF16)
                                BV = [Buf(f"V{g}") for g in range(NT // 8)]
                                Pb = [sb(ph, f"P{i}", [128, 1024], BF16) for i in range(4)]
                                BP = [Buf(f"P{i}") for i in range(4)]
                                Rhl = sb(ph, "Rhl", [128, 1024], BF16)
                                osb = sb(ph, "osb", [128, 512], F32)
                                Rs = sb(ph, "Rs", [128, 512], F32)
                                BRr, Bosb, BRs = Buf("Rr"), Buf("osb"), Buf("Rs")
                                K.memset(POOL, V[:, :, 64:65], 1.0, BV)
                                w_ukv = sb(ph, "w_ukv", [128, 2, 1024], BF16)
                                wi_ = 0
                                for m in range(2):
                                    for hf in range(2):
                                        stg_ = Pb[wi_][:].bitcast(F32)
                                        K.dma(SP, stg_, w_ukvp_d[m * 128:(m + 1) * 128, hf * 512:(hf + 1) * 512], (), [BP[wi_]])
                                        if wi_ % 2 == 0:
                                            K.ts(DVE, w_ukv[:, m, hf * 512:(hf + 1) * 512], stg_, C.kvv[:, m:m + 1], ALU.mult,
                                                 [BP[wi_]], [Bwukv])
                                        else:
                                            K.act(w_ukv[:, m, hf * 512:(hf + 1) * 512], stg_, AF.Copy, [BP[wi_]], [Bwukv],
                                                  scale=C.kvv[:, m:m + 1])
                                        wi_ += 1
                                NSC = 3
                                sc = [pp[0], pp[1], pp[2]]
                                Bsc = [[Bbank[0], Bbank[1]], [Bbank[2], Bbank[3]], [Bbank[4], Bbank[5]]]
                                gen, Bgen = bank[7], Bbank[7]
                                accb_ = [6, 7, 6, 6]
                                fbb_ = [6, 7, 7, 7]
                                pi = 0
                                pending = []
                                lgb_ = [7, 7, 7]
                                lgi_ = [0]

                                def gen_kt(h, c, gen=gen, Bgen=Bgen):
                                    for m in range(2):
                                        K.mm(gen, w_ukv[:, m, h * 128:(h + 1) * 128], kvnT[:, m, c * 512:(c + 1) * 512],
                                             m == 0, m == 1, [Bwukv, Bkvn[c]], [Bgen])
                                    K.cp(DVE, KT[64:128, c * 512:(c + 1) * 512], gen[64:128, :], [Bgen], [BKTn[c]])

                                def gen_kt2(h, c, gen=gen, Bgen=Bgen):
                                    wk = w_ukv[:, :, h * 128 + 64:h * 128 + 128]
                                    for m in range(2):
                                        for t_ in range(2):
                                            cc_ = c + t_
                                            K.mm(gen[64 * t_:64 * t_ + 64, :], wk[:, m, :], kvnT[:, m, cc_ * 512:(cc_ + 1) * 512],
                                                 m == 0, m == 1, [Bwukv, Bkvn[cc_]], [Bgen], tile_position=(0, 64 * t_))
                                    K.cp(DVE, KT[64:128, c * 512:(c + 1) * 512], gen[0:64, :], [Bgen], [BKTn[c]])
                                    K.cp(DVE, KT[64:128, (c + 1) * 512:(c + 2) * 512], gen[64:128, :], [Bgen], [BKTn[c + 1]])

                                def gen_v(h, g8, gen=gen, Bgen=Bgen):
                                    for j in range(8):
                                        kt = g8 * 8 + j
                                        for m in range(2):
                                            K.mm(gen[:, j * 64:(j + 1) * 64], kvnT[:, m, kt * 128:(kt + 1) * 128],
                                                 w_ukv[:, m, h * 128:h * 128 + 64], m == 0, m == 1,
                                                 [Bwukv, Bkvn[kt // 4]], [Bgen])
                                    K.cp(DVE, V[:, g8 * 8:(g8 + 1) * 8, 0:64],
                                         gen.rearrange("p (j d) -> p j d", j=8), [Bgen], [BV[g8]])

                                gb_ = [0, 1, 2, 3, 4, 5, 7]
                                gi_ = 0
                                for c in range(0, NCK, 2):
                                    b_ = gb_[gi_ % len(gb_)]
                                    gi_ += 1
                                    gen_kt2(0, c, bank[b_], Bbank[b_])
                                for g8 in range(NT // 8):
                                    b_ = gb_[gi_ % len(gb_)]
                                    gi_ += 1
                                    gen_v(0, g8, bank[b_], Bbank[b_])
                                NG = NT // 2
                                NQC = QB // 512
                                for h in range(8):
                                    groups = [(qc, g) for qc in range(NQC) for g in range(NG)]
                                    kt_done, v_done = [0], [0]

                                    lastgen = h + 1 < 8
                                    slots_ = [(idx_ % NSC if (lastgen and q_ == NQC - 1) else idx_ % NSC)
                                              for idx_, (q_, g_) in enumerate(groups)]
                                    nq = [0]

                                    def emit_qk(idx):
                                        qc, g = groups[idx]
                                        s_ = slots_[idx]
                                        for j in range(2):
                                            kt = 2 * g + j
                                            K.mm(sc[s_][:, j * 512:(j + 1) * 512], KT[:, kt * 128:(kt + 1) * 128],
                                                 QT[:, h, qc * 512:(qc + 1) * 512], True, True,
                                                 [BKTr[kt // 4], BKTn[kt // 4], BQT[qc][h]], [Bsc[s_][j]])

                                    def advance_qk(idx):
                                        while nq[0] < len(groups) and nq[0] <= idx + NSC - 1 and \
                                                all(slots_[k] != slots_[nq[0]] for k in range(idx, nq[0])):
                                            emit_qk(nq[0])
                                            nq[0] += 1

                                    for idx, (qc, g) in enumerate(groups):
                                        s_ = slots_[idx]
                                        acc, Bacc = bank[accb_[qc]], Bbank[accb_[qc]]
                                        advance_qk(idx)
                                        p_ = pi % 4
                                        pi += 1
                                        K.act(Pb[p_][:], sc[s_][:], AF.Exp, Bsc[s_], [BP[p_]])
                                        for j in range(2):
                                            kt = 2 * g + j
                                            K.mm(acc[0:65, :], V[:, kt, 0:65], Pb[p_][:, j * 512:(j + 1) * 512],
                                                 g == 0 and j == 0, g == NG - 1 and j == 1, [BV[kt // 8], BP[p_]],
                                                 [Bacc])
                                        if g == NG - 1:
                                            ob = (h % 2) * 64
                                            K.cp(DVE, osb[ob:ob + 64, :], acc[0:64, :], [Bacc], [Bosb])
                                            K.cp(DVE, Rs[64:65, :], acc[64:65, :], [Bacc], [BRs])
                                            K.recip(Rs[64:65, :], Rs[64:65, :], [BRs], [BRs])
                                            K.cp(DVE, Rhl[64:65, 0:512], Rs[64:65, :], [BRs], [BRr])
                                            K.tt(DVE, Rhl[64:65, 512:1024], Rs[64:65, :], Rhl[64:65, 0:512], ALU.subtract,
                                                 [BRs, BRr], [BRr])
                                            pending.append((h, blk * NQC + qc, fbb_[qc]))
                                        if pending and (g == min(6, NG - 2) or (h == 7 and qc == NQC - 1 and g == NG - 1)):
                                            fh, qcg, fb_ = pending.pop(0)
                                            ob = (fh % 2) * 64
                                            fbk, Bfbk = bank[fb_], Bbank[fb_]
                                            K.mm(fbk, C.ones[64:65, 0:128], Rhl[64:65, 0:512], True, False, [C.Bconst, BRr], [Bfbk])
                                            K.mm(fbk, C.ones[64:65, 0:128], Rhl[64:65, 512:1024], False, True, [C.Bconst, BRr], [Bfbk])
                                            K.tt(DVE, oT[ob:ob + 64, fh // 2, qcg * 512:(qcg + 1) * 512],
                                                 fbk[ob:ob + 64, :], osb[ob:ob + 64, :], ALU.mult, [Bfbk, Bosb], [BoT[qcg]])
                                        if h + 1 < 8 and qc == NQC - 1:
                                            ge = min(nq[0] - 1 - qc * NG, NG - 1)
                                            while kt_done[0] + 2 <= (ge + 1) // 2:
                                                b_ = lgb_[lgi_[0] % 3]
                                                lgi_[0] += 1
                                                gen_kt2(h + 1, kt_done[0], bank[b_], Bbank[b_])
                                                kt_done[0] += 2
                                            while v_done[0] < (g + 1) // 4:
                                                b_ = lgb_[lgi_[0] % 3]
                                                lgi_[0] += 1
                                                gen_v(h + 1, v_done[0], bank[b_], Bbank[b_])
                                                v_done[0] += 1
                                T.flush()

                with contextlib.ExitStack() as ph:
                    NCH = NQ // 512
                    w3 = sb(ph, "w3", [128, 8, 2560], BF16)
                    w_out = sb(ph, "w_out", [128, 8, D], BF16)
                    npost = sb(ph, "npost", [128, D], F32)
                    Bw3, Bwout, Bnpost = Buf("w3"), Buf("wout"), Buf("npost")
                    L = LNT(ph, 4)
                    xnT = [sb(ph, f"xnT{i}", [128, 8, 512], BF16) for i in range(3)]
                    BxnT = [Buf(f"xnT{i}") for i in range(3)]
                    xnTh = sb(ph, "xnTh", [128, 8, 128], BF16)
                    BxnTh = Buf("xnTh")
                    cu = sb(ph, "cu", [128, 514], F32)
                    hcT = sb(ph, "hcT", [128, 4, 16], F32)
                    cuh = sb(ph, "cuh", [128, 4, 16], F32)
                    Bcuh = Buf("cuh")
                    cT = sb(ph, "cT", [128, 512], F32)
                    sg = [sb(ph, f"sg{i}", [128, 512], F32) for i in range(2)]
                    szt = [sb(ph, f"szt{i}", [128, 512], F32) for i in range(2)]
                    bzt2 = [sb(ph, f"bzt{i}", [128, 512], F32) for i in range(2)]
                    Bbzt2 = [Buf("bzt0"), Buf("bzt1")]
                    a0 = [sb(ph, f"a0{i}", [128, 512], F32) for i in range(2)]
                    mix = [sb(ph, f"mix{i}", [128, 8, 512], BF16) for i in range(2)]
                    y1 = sb(ph, "y1", [128, D], F32)
                    yo = [sb(ph, f"yo{i}", [128, D], F32) for i in range(2)]
                    xres = [sb(ph, f"xres{i}", [128, D], F32) for i in range(2)]
                    Bcu, Bhc, BcT, Bbzt, By1 = (Buf(n) for n in ("cu", "hc", "cT", "bzt", "y1"))
                    Bsg = [Buf("sg0"), Buf("sg1")]
                    Bszt = [Buf("szt0"), Buf("szt1")]
                    Ba0 = [Buf("a00"), Buf("a01")]
                    Bmix = [[Buf(f"mix{i}_{k}") for k in range(8)] for i in range(2)]
                    Byo = [Buf("yo0"), Buf("yo1")]
                    Bxres = [Buf("xres0"), Buf("xres1")]
                    stg3 = [t_[:] for t_ in L.xt] + [t_[:] for t_ in xres] + [t_[:] for t_ in yo] + [y1[:]]
                    Bstg3 = L.Bxt + Bxres + Byo + [By1]
                    wctr = [0]
                    for cb in range(0, 2560, 1024):
                        ncol = min(1024, 2560 - cb)
                        K.load_fold(w3, Bw3, w3_d, 8, ncol, C.npre, stg3, Bstg3, col0=cb, ctr=wctr)
                    K.load_fold(w_out, Bwout, w_out_d, 8, D, None, stg3, Bstg3, ctr=wctr)
                    K.dma(SP, npost[:], npost_d.partition_broadcast(128), (), [Bnpost])
                    hr = job["halo"]
                    nh2 = 2 * NCH
                    hs = L.front(xh[hr:hr + nh2, :], nrows=nh2, pre_zero=True)
                    L.back(hs, xnTh, BxnTh, 0, bank[0], Bbank[0])
                    hv = bank[1].rearrange("p (g c) -> p g c", g=8)
                    for cc in range(4):
                        for ty, col0 in ((0, 0), (1, 1024)):
                            for k in range(8):
                                K.mm(hv[:, cc * 2 + ty, 0:nh2], w3[:, k, col0 + cc * 128:col0 + (cc + 1) * 128],
                                     xnTh[:, k, 0:nh2], k == 0, k == 7, [Bw3, BxnTh], [Bbank[1]])
                    hv4 = bank[1].rearrange("p (cc t c) -> p cc t c", cc=4, t=2)
                    K.cp(DVE, hcT[:, :, 0:nh2], hv4[:, :, 1, 0:nh2], [Bbank[1]], [Bhc])
                    K.tt(DVE, cuh[:, :, 0:nh2], hv4[:, :, 0, 0:nh2], hcT[:, :, 0:nh2], ALU.mult, [Bbank[1], Bhc], [Bcuh])
                    sgc = [0]
                    yctr = [0]
                    octr = [0]

                    def silu_gate(bk, Bbk):
                        i = sgc[0] % 2
                        sgc[0] += 1
                        K.act(sg[i][:], bk, AF.Exp, [Bbk], [Bsg[i]], scale=-1.0)
                        K.act(sg[i][:], sg[i][:], AF.Ln, [Bsg[i]], [Bsg[i]], bias=C.onesf[:, 0:1])
                        K.act(sg[i][:], sg[i][:], AF.Exp, [Bsg[i]], [Bsg[i]], scale=-1.0)
                        K.tt(DVE, szt[i][:], bk, sg[i][:], ALU.mult, [Bbk, Bsg[i]], [Bszt[i]])
                        return szt[i], Bszt[i]

                    lslots = {}

                    def st3_lnt_front(j):
                        lslots[j] = L.chunk_front(lambda t: job["xq"][j * 512 + t * 128:j * 512 + (t + 1) * 128, :])

                    def st3_lnt_back(j):
                        L.chunk_back(lslots.pop(j), xnT[j % 3], BxnT[j % 3], bank[0:2], Bbank[0:2])

                    def st3_lnt(j):
                        st3_lnt_front(j)
                        st3_lnt_back(j)

                    def st3_units(j):
                        X, BX = xnT[j % 3], BxnT[j % 3]
                        mx, Bmx = mix[j % 2], Bmix[j % 2]
                        if j > 0:
                            Lh, BLh, lc = xnT[(j - 1) % 3], BxnT[(j - 1) % 3], 511
                        else:
                            Lh, BLh, lc = xnTh, BxnTh, 0
                        if j + 1 < NCH:
                            Rh, BRh, rcol = xnT[(j + 1) % 3], BxnT[(j + 1) % 3], 0
                        else:
                            Rh, BRh, rcol = xnTh, BxnTh, 1

                        def mm_uc(cc):
                            for bi, col0 in ((2, 0), (3, 1024)):
                                for k in range(8):
                                    K.mm(bank[bi], w3[:, k, col0 + cc * 128:col0 + (cc + 1) * 128], X[:, k, :], k == 0,
                                         k == 7, [Bw3, BX], [Bbank[bi]])

                        def ew_uc(cc):
                            a, Ba = a0[cc % 2], Ba0[cc % 2]
                            K.cp(ACT, cT[:], bank[3], [Bbank[3]], [BcT])
                            K.tt(DVE, cu[:, 1:513], bank[2], cT[:], ALU.mult, [Bbank[2], BcT], [Bcu])
                            K.cp(DVE, cu[:, 0:1], cuh[:, cc, 2 * j:2 * j + 1], [Bcuh], [Bcu])
                            K.cp(DVE, cu[:, 513:514], cuh[:, cc, 2 * j + 1:2 * j + 2], [Bcuh], [Bcu])
                            K.ts(DVE, a[:], cu[:, 1:513], C.convw[:, cc * 3 + 1:cc * 3 + 2], ALU.mult, [Bcu, C.Bconst], [Ba])
                            K.stt(a[:], cu[:, 0:512], C.convw[:, cc * 3:cc * 3 + 1], a[:], ALU.mult, ALU.add,
                                  [Bcu, Ba, C.Bconst], [Ba])
                            K.stt(a[:], cu[:, 2:514], C.convw[:, cc * 3 + 2:cc * 3 + 3], a[:], ALU.mult, ALU.add,
                                  [Bcu, Ba, C.Bconst], [Ba])

                        def mm_bz(cc):
                            for bi, col0 in ((4, 512), (5, 1536)):
                                for k in range(8):
                                    K.mm(bank[bi], w3[:, k, col0 + cc * 128:col0 + (cc + 1) * 128], X[:, k, :], k == 0,
                                         k == 7, [Bw3, BX], [Bbank[bi]])

                        def ew_bz(cc):
                            a, Ba = a0[cc % 2], Ba0[cc % 2]
                            sz_, Bsz_ = silu_gate(bank[5], Bbank[5])
                            bzt, Bbzt_ = bzt2[cc % 2], Bbzt2[cc % 2]
                            K.tt(DVE, bzt[:], bank[4], sz_[:], ALU.mult, [Bbank[4], Bsz_], [Bbzt_])
                            K.tt(POOL, mx[:, cc, :], a[:], bzt[:], ALU.mult, [Ba, Bbzt_], [Bmx[cc]])

                        zab = (2, 4, 3, 5)

                        def mm_za(p):
                            bi = zab[p]
                            for k in range(8):
                                K.mm(bank[bi], w3[:, k, 2048 + p * 128:2048 + (p + 1) * 128], X[:, k, :], k == 0, k == 7,
                                     [Bw3, BX], [Bbank[bi]])

                        def ew_za(p):
                            bi = zab[p]
                            sz_, Bsz_ = silu_gate(bank[bi], Bbank[bi])
                            K.tt(POOL, mx[:, 4 + p, :], oT[:, p, j * 512:(j + 1) * 512], sz_[:], ALU.mult,
                                 [BoT[j], Bsz_], [Bmx[4 + p]])

                        units = []
                        for cc in range(4):
                            units.append((mm_uc, ew_uc, cc))
                            units.append((mm_bz, ew_bz, cc))
                        for p in range(4):
                            units.append((mm_za, ew_za, p))
                        units[0][0](units[0][2])
                        for ui, (mmf, ewf, arg) in enumerate(units):
                            if ui + 1 < len(units):
                                units[ui + 1][0](units[ui + 1][2])
                            ewf(arg)

                    obanks = (6, 7)

                    def st3_out(j):
                        mx, Bmx = mix[j % 2], Bmix[j % 2]
                        for t in range(4):
                            r0 = j * 512 + t * 128
                            xr, Bxr = xres[yctr[0] % 2], Bxres[yctr[0] % 2]
                            yb, Byb = yo[yctr[0] % 2], Byo[yctr[0] % 2]
                            i8 = yctr[0] % 4
                            yctr[0] += 1
                            K.dma(SP, xr[:], job["xq"][r0:r0 + 128, :], (), [Bxr])
                            ssa, ssb2 = C.ss[:, i8 * 2:i8 * 2 + 1], C.ss[:, i8 * 2 + 1:i8 * 2 + 2]
                            rsc = C.rs[:, i8 * 2:i8 * 2 + 1]
                            Bssa, Bssb2, Brsc = C.Bss[i8 * 2], C.Bss[i8 * 2 + 1], C.Brs[i8 * 2]
                            for nh in range(2):
                                bi = obanks[octr[0] % 2]
                                octr[0] += 1
                                for k in range(8):
                                    K.mm(bank[bi], mx[:, k, t * 128:(t + 1) * 128], w_out[:, k, nh * 512:(nh + 1) * 512],
                                         k == 0, k == 7, [Bmx[k], Bwout], [Bbank[bi]])
                                K.act(yb[:, nh * 512:(nh + 1) * 512], bank[bi], AF.Square, [Bbank[bi]],
                                      [Byb, Bssa if nh == 0 else Bssb2], accum=ssa if nh == 0 else ssb2)
                                K.tt(DVE, y1[:, nh * 512:(nh + 1) * 512], bank[bi], npost[:, nh * 512:(nh + 1) * 512],
                                     ALU.mult, [Bbank[bi], Bnpost], [By1])
                            K.tt(DVE, ssa, ssa, ssb2, ALU.add, [Bssa, Bssb2], [Bssa])
                            K.act(rsc, ssa, AF.Ln, [Bssa], [Brsc], scale=1.0 / D, bias=C.epsb[:, 0:1])
                            K.act(rsc, rsc, AF.Exp, [Brsc], [Brsc], scale=-0.5)
                            K.stt(yb[:], y1[:], rsc, xr[:], ALU.mult, ALU.add, [By1, Brsc, Bxr], [Byb])
                            K.dma(POOL, job["y"][r0:r0 + 128, :], yb[:], [Byb], ())

                    st3_lnt(0)
                    if NCH > 1:
                        st3_lnt(1)
                    for j in range(NCH + 1):
                        if j + 2 < NCH:
                            st3_lnt_front(j + 2)
                        if j - 1 >= 0:
                            st3_out(j - 1)
                        if j < NCH:
                            st3_units(j)
                        if j + 2 < NCH:
                            st3_lnt_back(j + 2)
                    T.flush(final=(job is jobs[-1]))
    return nc


def _rope_tables(S):
    freqs = (1.0 / (np.float32(10000.0) ** (np.arange(0, 32, 2, dtype=np.float32) / np.float32(32)))).astype(np.float32)
    ang = (np.arange(S, dtype=np.float32)[:, None] * freqs[None, :]).astype(np.float32)
    cos = np.cos(ang).astype(np.float32).T
    sin = np.sin(ang).astype(np.float32).T
    tab = np.empty((2, 32, S), np.float32)
    tab[0, 0:16] = cos
    tab[0, 16:32] = cos
    tab[1, 0:16] = -sin
    tab[1, 16:32] = sin
    return tab


_PROGRAM = None


def kernel(x_prompt, x_sample, norm_pre, w_in, conv_w, q_norm, w_uq, kv_norm, w_ukv, w_out, norm_post):
    global _PROGRAM
    f = np.float32
    x_prompt = np.asarray(x_prompt, f)
    x_sample = np.asarray(x_sample, f)
    w_in = np.asarray(w_in, f)[0]
    w_uq = np.asarray(w_uq, f)[0]
    w_ukv = np.asarray(w_ukv, f)[0]
    w_out_ = np.asarray(w_out, f)[0]
    norm_pre = np.asarray(norm_pre, f)[0]
    q_norm = np.asarray(q_norm, f)[0]
    kv_norm = np.asarray(kv_norm, f)[0]
    norm_post = np.asarray(norm_post, f)[0]
    conv_w = np.asarray(conv_w, f)[0]

    kpe = w_in[:, 2688:2720]
    w_kvp = np.concatenate([w_in[:, 2432:2688], kpe, kpe[:, 16:32], kpe[:, 0:16]], axis=1)
    w_ql = w_in[:, 2048:2432]
    uq = w_uq.reshape(384, 8, 96)
    rope_c = uq[:, :, 64:96]
    swap_c = np.concatenate([uq[:, :, 80:96], uq[:, :, 64:80]], axis=2)
    nope_c = uq[:, :, 0:64]
    w_uqp = np.concatenate([rope_c[:, 0:4].reshape(384, 128), rope_c[:, 4:8].reshape(384, 128),
                            swap_c[:, 0:4].reshape(384, 128), swap_c[:, 4:8].reshape(384, 128),
                            nope_c.reshape(384, 512)], axis=1)
    ukv = w_ukv.reshape(256, 8, 128)
    w_ukvp = np.concatenate([ukv[:, :, 64:128], ukv[:, :, 0:64]], axis=2).reshape(256, 1024)
    w3 = np.concatenate([w_in[:, 0:2048], w_in[:, 2720:3232]], axis=1)
    npre = norm_pre.reshape(8, 128).T
    qnv = q_norm.reshape(3, 128).T
    kvv = kv_norm.reshape(2, 128).T
    convw = conv_w.reshape(3, 4, 128).transpose(2, 1, 0).reshape(128, 12)
    rope = _rope_tables(S_S)
    shared = dict(
        rope=rope, ropep4=np.tile(rope[:, :, 0:S_P], (1, 4, 1)), w_kvp=w_kvp, w_ql=w_ql, w_uqp=w_uqp, w_ukvp=w_ukvp, w3=w3, w_out=w_out_, npre=npre, qnv=qnv,
        kvv=kvv, convw=convw, npost=norm_post.reshape(1, D), ident=np.eye(128, dtype=f))
    shared = {k: np.ascontiguousarray(v, dtype=f) for k, v in shared.items()}
    in_maps = []
    for c in range(NCORES):
        sidx, q0 = c // 4, (c % 4) * NQ_S
        xs = x_sample[sidx]
        xh = np.zeros((24, D), f)
        xpc = x_prompt[c]
        for j in range(S_P // 512):
            if j * 512 - 1 >= 0:
                xh[2 * j] = xpc[j * 512 - 1]
            if (j + 1) * 512 < S_P:
                xh[2 * j + 1] = xpc[(j + 1) * 512]
        for j in range(NQ_S // 512):
            lo_, hi_ = q0 + j * 512 - 1, q0 + (j + 1) * 512
            if lo_ >= 0:
                xh[8 + 2 * j] = xs[lo_]
            if hi_ < S_S:
                xh[8 + 2 * j + 1] = xs[hi_]
        m = dict(shared)
        m["xp"] = np.ascontiguousarray(x_prompt[c])
        m["xs"] = np.ascontiguousarray(xs)
        m["xq"] = np.ascontiguousarray(xs[q0:q0 + NQ_S])
        m["xh"] = xh
        m["ropeq"] = np.ascontiguousarray(np.tile(rope[:, :, q0:q0 + NQ_S], (1, 4, 1)))
        in_maps.append(m)
    if _PROGRAM is None:
        _PROGRAM = build_program()
    res = run_bass_kernel_spmd(_PROGRAM, in_maps, core_ids=list(range(NCORES)))
    y_prompt = np.stack([np.asarray(res.results[c]["yp"], f) for c in range(NCORES)], axis=0)
    y_sample = np.empty((2, S_S, D), f)
    for c in range(NCORES):
        sidx, q0 = c // 4, (c % 4) * NQ_S
        y_sample[sidx, q0:q0 + NQ_S] = np.asarray(res.results[c]["ys"], f)
    return (y_prompt, y_sample)
```
